# Optimizing a Trainium2 kernel written in Bass

```python
import math
import jax, jax.numpy as jnp
from jax import lax
import numpy as np

D_MODEL = 2048
BATCH = 16
SEQ = 256
DEPTH = 2
DEC_BATCH = 4
DEC_SEQ = 1024
PAST_LEN = 256

GRID_W = 64
W_A = 1024
N_BLOCKS = 16
BLOCK_W = W_A // N_BLOCKS
CONV_A = 4
RG_C = 8.0
W_B = 1024
CONV_B = 3
HY_BANDS = 16
HY_EMB = 2 * HY_BANDS + 1
HY_HIDDEN = 64
N_EXPERTS = 32
TOP_K = 4
D_FF = 2048
SWIGLU_LIMIT = 7.0
SWIGLU_ALPHA = 1.702
MOE_BLOCK = 128
N_MOD = 6
IN_WIDTH = 2 * W_A + 3 * W_B
DN_ALPHA = (2 * DEPTH) ** 0.25
DN_BETA = (8 * DEPTH) ** -0.25
LN_EPS = 1e-5

kernel_name = 'hybrid_rglru_hyena_moe_diffusion_step'

F32 = jnp.float32


def _layer_norm(x, g, b):
    xf = x.astype(F32)
    mu = xf.mean(-1, keepdims=True)
    var = jnp.square(xf - mu).mean(-1, keepdims=True)
    return ((xf - mu) * lax.rsqrt(var + LN_EPS)).astype(x.dtype) * g + b


def _dw_conv(x, w, b, left):
    k_w, ch = w.shape
    y = lax.conv_general_dilated(x, w[:, None, :].astype(x.dtype), window_strides=(1,),
                                 padding=[(left, k_w - 1 - left)],
                                 dimension_numbers=('NWC', 'WIO', 'NWC'),
                                 feature_group_count=ch)
    return y + b


def _block_diag(u, w, b):
    bs, sl, _ = u.shape
    ub = u.reshape(bs, sl, N_BLOCKS, BLOCK_W)
    return jnp.einsum('blnc,ncd->blnd', ub, w).reshape(bs, sl, W_A) + b


def _linear_recurrence(left, right):
    a1, b1 = left
    a2, b2 = right
    return a1 * a2, a2 * b1 + b2


def _rglru_scan(u, wr, br, wi, bi, lam, h0, reverse):
    r = jax.nn.sigmoid(_block_diag(u, wr, br).astype(F32))
    gi = jax.nn.sigmoid(_block_diag(u, wi, bi).astype(F32))
    log_a = -RG_C * r * jax.nn.softplus(-lam.astype(F32))
    a = jnp.exp(log_a)
    b = jnp.sqrt(-jnp.expm1(2.0 * log_a)) * (gi * u.astype(F32))
    edge = -1 if reverse else 0
    b = b.at[:, edge].add(a[:, edge] * h0.astype(F32))
    _, h = lax.associative_scan(_linear_recurrence, (a, b), reverse=reverse, axis=1)
    final = h[:, 0] if reverse else h[:, -1]
    return h, final


def _hyena_filter(n_tok, w1, b1, freq, w2, b2, w3, b3, decay):
    t = jnp.arange(n_tok, dtype=F32)
    tn = t / (n_tok - 1)
    bands = jnp.linspace(1e-4, HY_BANDS - 1, HY_BANDS, dtype=F32)
    ang = (2.0 * math.pi / n_tok) * t[:, None] * bands
    z = jnp.concatenate([tn[:, None], jnp.cos(ang), -jnp.sin(ang)], axis=-1)
    f = freq.astype(F32)
    hdn = jnp.sin(f * (z @ w1.astype(F32) + b1.astype(F32)))
    hdn = jnp.sin(f * (hdn @ w2.astype(F32) + b2.astype(F32)))
    filt = (hdn @ w3.astype(F32) + b3.astype(F32)) * jnp.exp(-tn[:, None] * jnp.abs(decay.astype(F32)))
    h_fwd, h_bwd = filt[:, :W_B], filt[:, W_B:]
    return jnp.concatenate([h_fwd, jnp.zeros((1, W_B), F32), h_bwd[:0:-1]], axis=0)


def _long_conv(u, filt, bias):
    n_tok = u.shape[1]
    uf = jnp.fft.rfft(u.astype(F32), n=2 * n_tok, axis=1)
    kf = jnp.fft.rfft(filt, n=2 * n_tok, axis=0)
    y = jnp.fft.irfft(uf * kf[None], n=2 * n_tok, axis=1)[:, :n_tok]
    return (y + u.astype(F32) * bias.astype(F32)).astype(u.dtype)


def _mixer(h, h0, lp):
    n_tok = h.shape[1]
    proj = h @ lp['w_in']
    xa, ga, hy = proj[..., :W_A], proj[..., W_A:2 * W_A], proj[..., 2 * W_A:]
    u = _dw_conv(xa, lp['conv_a_w'], lp['conv_a_b'], CONV_A // 2)
    y_f, s_f = _rglru_scan(u, lp['rg_wr'][0], lp['rg_br'][0], lp['rg_wi'][0], lp['rg_bi'][0],
                           lp['rg_lambda'][0], h0[:, 0], False)
    y_r, s_r = _rglru_scan(u, lp['rg_wr'][1], lp['rg_br'][1], lp['rg_wi'][1], lp['rg_bi'][1],
                           lp['rg_lambda'][1], h0[:, 1], True)
    y_a = ((y_f + y_r) * jax.nn.gelu(ga.astype(F32))).astype(h.dtype)
    hy = _dw_conv(hy, lp['conv_b_w'], lp['conv_b_b'], CONV_B // 2)
    v, x0, x1 = hy[..., :W_B], hy[..., W_B:2 * W_B], hy[..., 2 * W_B:]
    filt = _hyena_filter(n_tok, lp['hy_w1'], lp['hy_b1'], lp['hy_freq'], lp['hy_w2'], lp['hy_b2'],
                         lp['hy_w3'], lp['hy_b3'], lp['hy_decay'])
    y_b = x0 * _long_conv(v * x1, filt, lp['hy_bias'])
    gates = jax.nn.sigmoid(h @ lp['w_gate'] + lp['b_gate'])
    merged = (gates[..., :D_MODEL] * (y_a @ lp['w_proj_a'])
              + gates[..., D_MODEL:] * (y_b @ lp['w_proj_b']))
    return merged @ lp['w_out'] + lp['b_out'], jnp.stack([s_f, s_r], axis=1)


def _moe(h, router_w, router_b, w_gu, b_gu, w_down, b_down):
    bs, sl, dm = h.shape
    xt = h.reshape(bs * sl, dm)
    n_tok = xt.shape[0]
    n_slot = n_tok * TOP_K
    n_blk = -(-n_slot // MOE_BLOCK) + N_EXPERTS
    logits = (xt @ router_w + router_b).astype(F32)
    top_val, top_idx = lax.top_k(logits, TOP_K)
    gate_w = jax.nn.softmax(top_val, axis=-1).reshape(n_slot)
    flat_e = top_idx.reshape(n_slot)
    order = jnp.argsort(flat_e)
    sorted_e = flat_e[order]
    tok = order // TOP_K
    sizes = jnp.bincount(flat_e, length=N_EXPERTS)
    padded = (sizes + MOE_BLOCK - 1) // MOE_BLOCK * MOE_BLOCK
    pad_end = jnp.cumsum(padded)
    rank = jnp.arange(n_slot) - (jnp.cumsum(sizes) - sizes)[sorted_e]
    dest = (pad_end - padded)[sorted_e] + rank
    x_pad = jnp.zeros((n_blk * MOE_BLOCK, dm), xt.dtype).at[dest].set(xt[tok])
    blk_start = jnp.arange(n_blk) * MOE_BLOCK
    blk_e = jnp.minimum(jnp.sum(blk_start[:, None] >= pad_end[None, :], axis=1), N_EXPERTS - 1)

    def expert_block(args):
        xb, e = args
        gu = xb @ w_gu[e] + b_gu[e]
        g = jnp.minimum(gu[:, :D_FF], SWIGLU_LIMIT)
        lin = jnp.clip(gu[:, D_FF:], -SWIGLU_LIMIT, SWIGLU_LIMIT)
        act = g * jax.nn.sigmoid(SWIGLU_ALPHA * g) * (lin + 1.0)
        return act @ w_down[e] + b_down[e]

    y_pad = lax.map(expert_block, (x_pad.reshape(n_blk, MOE_BLOCK, dm), blk_e))
    y_slot = y_pad.reshape(n_blk * MOE_BLOCK, dm)[dest] * gate_w[order][:, None].astype(h.dtype)
    y = jax.ops.segment_sum(y_slot, tok, num_segments=n_tok)
    return y.reshape(bs, sl, dm)


def _layer(x, cond, h0, lp):
    mod = (jax.nn.silu(cond) @ lp['w_mod'] + lp['b_mod']).reshape(cond.shape[0], 1, N_MOD, D_MODEL)
    sh1, sc1, g1, sh2, sc2, g2 = [mod[:, :, i] for i in range(N_MOD)]
    y, state = _mixer(x * (1.0 + sc1) + sh1, h0, lp)
    x = _layer_norm(DN_ALPHA * x + g1 * y, lp['ln1_g'], lp['ln1_b'])
    y = _moe(x * (1.0 + sc2) + sh2, lp['router_w'], lp['router_b'], lp['w_gu'], lp['b_gu'],
             lp['w_down'], lp['b_down'])
    x = _layer_norm(DN_ALPHA * x + g2 * y, lp['ln2_g'], lp['ln2_b'])
    return x, state


def _trunk(x, cond, h0_all, params, keep_state):
    states = []
    for l in range(DEPTH):
        lp = {name: w[l] for name, w in params.items()}
        x, s = _layer(x, cond, h0_all[:, l], lp)
        if keep_state:
            states.append(s)
    return x, states


def _grid_pos_embed(n_tok):
    rows = n_tok // GRID_W
    row = jnp.repeat(jnp.arange(rows, dtype=F32), GRID_W)
    col = jnp.tile(jnp.arange(GRID_W, dtype=F32), rows)
    q = D_MODEL // 4
    omega = 1.0 / (10000.0 ** (jnp.arange(q, dtype=F32) / q))
    er = row[:, None] * omega
    ec = col[:, None] * omega
    return jnp.concatenate([jnp.sin(er), jnp.cos(er), jnp.sin(ec), jnp.cos(ec)], axis=-1)


def setup_inputs(seed: int = 0) -> dict:
    key = jax.random.key(seed)
    ks = iter(jax.random.split(key, 48))

    def nrm(shape, scale):
        return jax.random.normal(next(ks), shape, F32) * scale

    def unif(shape, lo, hi):
        return jax.random.uniform(next(ks), shape, F32, lo, hi)

    a0 = unif((DEPTH, 2, W_A), 0.9, 0.999)
    return {
        'x_prompt': nrm((BATCH, SEQ, D_MODEL), 1.0),
        'x_sample': nrm((DEC_BATCH, DEC_SEQ, D_MODEL), 1.0),
        'state_rglru': nrm((DEC_BATCH, DEPTH, 2, W_A), 0.5),
        'c': nrm((DEC_BATCH, D_MODEL), 1.0),
        'c_ctx': nrm((D_MODEL,), 1.0),
        'w_mod': nrm((DEPTH, D_MODEL, N_MOD * D_MODEL), 0.5 * D_MODEL ** -0.5),
        'b_mod': nrm((DEPTH, N_MOD * D_MODEL), 0.02),
        'w_in': nrm((DEPTH, D_MODEL, IN_WIDTH), D_MODEL ** -0.5),
        'conv_a_w': nrm((DEPTH, CONV_A, W_A), CONV_A ** -0.5),
        'conv_a_b': nrm((DEPTH, W_A), 0.02),
        'rg_wr': nrm((DEPTH, 2, N_BLOCKS, BLOCK_W, BLOCK_W), BLOCK_W ** -0.5),
        'rg_br': nrm((DEPTH, 2, W_A), 0.02),
        'rg_wi': nrm((DEPTH, 2, N_BLOCKS, BLOCK_W, BLOCK_W), BLOCK_W ** -0.5),
        'rg_bi': nrm((DEPTH, 2, W_A), 0.02),
        'rg_lambda': jnp.log(a0) - jnp.log1p(-a0),
        'conv_b_w': nrm((DEPTH, CONV_B, 3 * W_B), CONV_B ** -0.5),
        'conv_b_b': nrm((DEPTH, 3 * W_B), 0.02),
        'hy_w1': nrm((DEPTH, HY_EMB, HY_HIDDEN), HY_EMB ** -0.5),
        'hy_b1': nrm((DEPTH, HY_HIDDEN), 0.1),
        'hy_freq': 1.0 + nrm((DEPTH, HY_HIDDEN), 0.1),
        'hy_w2': nrm((DEPTH, HY_HIDDEN, HY_HIDDEN), HY_HIDDEN ** -0.5),
        'hy_b2': nrm((DEPTH, HY_HIDDEN), 0.1),
        'hy_w3': nrm((DEPTH, HY_HIDDEN, 2 * W_B), HY_HIDDEN ** -0.5),
        'hy_b3': nrm((DEPTH, 2 * W_B), 0.02),
        'hy_decay': unif((DEPTH, 2 * W_B), 3.07, 15.35),
        'hy_bias': nrm((DEPTH, W_B), 1.0),
        'w_proj_a': nrm((DEPTH, W_A, D_MODEL), DN_BETA * W_A ** -0.5),
        'w_proj_b': nrm((DEPTH, W_B, D_MODEL), DN_BETA * W_B ** -0.5),
        'w_gate': nrm((DEPTH, D_MODEL, 2 * D_MODEL), D_MODEL ** -0.5),
        'b_gate': nrm((DEPTH, 2 * D_MODEL), 0.02),
        'w_out': nrm((DEPTH, D_MODEL, D_MODEL), DN_BETA * D_MODEL ** -0.5),
        'b_out': nrm((DEPTH, D_MODEL), 0.02),
        'ln1_g': 1.0 + nrm((DEPTH, D_MODEL), 0.05),
        'ln1_b': nrm((DEPTH, D_MODEL), 0.02),
        'router_w': nrm((DEPTH, D_MODEL, N_EXPERTS), D_MODEL ** -0.5),
        'router_b': nrm((DEPTH, N_EXPERTS), 0.01),
        'w_gu': nrm((DEPTH, N_EXPERTS, D_MODEL, 2 * D_FF), DN_BETA * D_MODEL ** -0.5),
        'b_gu': nrm((DEPTH, N_EXPERTS, 2 * D_FF), 0.02),
        'w_down': nrm((DEPTH, N_EXPERTS, D_FF, D_MODEL), DN_BETA * D_FF ** -0.5),
        'b_down': nrm((DEPTH, N_EXPERTS, D_MODEL), 0.02),
        'ln2_g': 1.0 + nrm((DEPTH, D_MODEL), 0.05),
        'ln2_b': nrm((DEPTH, D_MODEL), 0.02),
    }


def reference(x_prompt, x_sample, state_rglru, c, c_ctx, w_mod, b_mod, w_in, conv_a_w, conv_a_b,
              rg_wr, rg_br, rg_wi, rg_bi, rg_lambda, conv_b_w, conv_b_b, hy_w1, hy_b1, hy_freq,
              hy_w2, hy_b2, hy_w3, hy_b3, hy_decay, hy_bias, w_proj_a, w_proj_b, w_gate, b_gate,
              w_out, b_out, ln1_g, ln1_b, router_w, router_b, w_gu, b_gu, w_down, b_down,
              ln2_g, ln2_b):
    params = dict(w_mod=w_mod, b_mod=b_mod, w_in=w_in, conv_a_w=conv_a_w, conv_a_b=conv_a_b,
                  rg_wr=rg_wr, rg_br=rg_br, rg_wi=rg_wi, rg_bi=rg_bi, rg_lambda=rg_lambda,
                  conv_b_w=conv_b_w, conv_b_b=conv_b_b, hy_w1=hy_w1, hy_b1=hy_b1, hy_freq=hy_freq,
                  hy_w2=hy_w2, hy_b2=hy_b2, hy_w3=hy_w3, hy_b3=hy_b3, hy_decay=hy_decay,
                  hy_bias=hy_bias, w_proj_a=w_proj_a, w_proj_b=w_proj_b, w_gate=w_gate,
                  b_gate=b_gate, w_out=w_out, b_out=b_out, ln1_g=ln1_g, ln1_b=ln1_b,
                  router_w=router_w, router_b=router_b, w_gu=w_gu, b_gu=b_gu, w_down=w_down,
                  b_down=b_down, ln2_g=ln2_g, ln2_b=ln2_b)
    h0_ctx = jnp.zeros((x_prompt.shape[0], DEPTH, 2, W_A), F32)
    y_prompt, ctx_states = _trunk(x_prompt, c_ctx[None, :], h0_ctx, params, True)
    new_state_rglru = jnp.stack(ctx_states, axis=1)
    x_lat = x_sample + _grid_pos_embed(x_sample.shape[1]).astype(x_sample.dtype)
    y_sample, _ = _trunk(x_lat, c, state_rglru, params, False)
    return (y_prompt, y_sample, new_state_rglru)
```

```python
import math
from contextlib import ExitStack
import numpy as np
import concourse.bass as bass
import concourse.mybir as mybir
from concourse.bass_utils import run_bass_kernel_spmd

F32 = mybir.dt.float32
BF16 = mybir.dt.bfloat16
AF = mybir.ActivationFunctionType
ALU = mybir.AluOpType
AX = mybir.AxisListType

D = 2048
T = 1024
NL = 2
NE = 32
ALPHA = (2 * NL) ** 0.25
EPS = 1e-5
PI_LO = 3.1415925


class V:
    def __init__(self, ap, keys):
        self.ap = ap
        self.keys = tuple(keys)

    def sub(self, ap):
        return V(ap, self.keys)


class Region:
    def __init__(self, name, ap, dtype, unit):
        self.name, self.ap, self.dtype, self.unit = name, ap, dtype, unit

    def view(self, c0, n, dtype=None):
        keys = [f"{self.name}.{u}" for u in range(c0 // self.unit, (c0 + n - 1) // self.unit + 1)]
        ap = self.ap[:, c0:c0 + n]
        if dtype is not None and dtype != self.dtype:
            ap = ap.bitcast(dtype)
        return V(ap, keys)


class Op:
    __slots__ = ("eng", "fn", "deps", "dma", "grp", "sig", "val", "waits", "sem")


class Prog:
    def __init__(self, nc, es):
        self.nc, self.es = nc, es
        self.ops = []
        self.last_w = {}
        self.readers = {}
        self.grp_sem = {}
        self.eng_sem = {}
        for e in ("pe", "act", "dve", "pool"):
            self.eng_sem[e] = es.enter_context(nc.semaphore("sem_" + e))
        self.out_groups = set()

    def add(self, eng, fn, r=(), w=(), dma=False, grp=None, is_out=False):
        idx = len(self.ops)
        deps = set()
        for k in r:
            d = self.last_w.get(k)
            if d is not None:
                deps.add(d)
        for k in w:
            d = self.last_w.get(k)
            if d is not None:
                deps.add(d)
            rd = self.readers.get(k)
            if rd:
                deps.update(rd.values())
        for k in r:
            rd = self.readers.setdefault(k, {})
            rd[("dma", idx) if dma else eng] = idx
        for k in w:
            self.last_w[k] = idx
            self.readers[k] = {}
        op = Op()
        op.eng, op.fn, op.dma, op.grp, op.sig, op.val, op.waits = eng, fn, dma, grp, dma, 0, None
        fdeps = []
        for d in deps:
            o = self.ops[d]
            if (not o.dma) and (not dma) and o.eng == eng and eng == "pe":
                continue
            if not o.dma:
                o.sig = True
            fdeps.append(d)
        op.deps = fdeps
        if dma:
            if grp not in self.grp_sem:
                self.grp_sem[grp] = self.es.enter_context(self.nc.semaphore("g_" + grp))
            if is_out:
                self.out_groups.add(grp)
        self.ops.append(op)
        return idx

    def emit(self):
        nc = self.nc
        cnt = {e: 0 for e in self.eng_sem}
        gcnt = {g: 0 for g in self.grp_sem}
        waited = {e: {} for e in ("pe", "act", "dve", "pool", "sp")}
        for op in self.ops:
            ws = {}
            for d in op.deps:
                o = self.ops[d]
                if o.dma:
                    key = ("g", o.grp)
                    val = gcnt[o.grp]
                else:
                    key = ("e", o.eng)
                    val = o.val
                if ws.get(key, 0) < val:
                    ws[key] = val
            wl = []
            wd = waited[op.eng]
            for key, val in ws.items():
                if wd.get(key, 0) >= val:
                    continue
                wd[key] = val
                sem = self.grp_sem[key[1]] if key[0] == "g" else self.eng_sem[key[1]]
                wl.append((sem, val))
            op.waits = wl
            if op.dma:
                gcnt[op.grp] += 16
                op.val = gcnt[op.grp]
                op.sem = self.grp_sem[op.grp]
            elif op.sig:
                cnt[op.eng] += 1
                op.val = cnt[op.eng]
                op.sem = self.eng_sem[op.eng]
        final = [(self.grp_sem[g], gcnt[g]) for g in sorted(self.out_groups)]
        with nc.Block() as block:
            decos = {"pe": block.tensor, "act": block.scalar, "dve": block.vector,
                     "pool": block.gpsimd, "sp": block.sync}
            for ename, deco in decos.items():
                ops = [o for o in self.ops if o.eng == ename]

                def body(e, ops=ops, ename=ename):
                    for o in ops:
                        for sem, val in o.waits:
                            e.wait_ge(sem, val)
                        ins = o.fn(e)
                        if o.sig:
                            ins.then_inc(o.sem, 16 if o.dma else 1)
                    if ename == "sp":
                        for sem, val in final:
                            e.wait_ge(sem, val)
                deco(body)


def _keys(*vs):
    ks = []
    for v in vs:
        if isinstance(v, V):
            ks.extend(v.keys)
    return ks


def _ap(v):
    return v.ap if isinstance(v, V) else v


class K:
    def __init__(self, P):
        self.P = P

    def mm(self, out, lhsT, rhs, start, stop):
        o, l, r = out.ap, lhsT.ap, rhs.ap
        self.P.add("pe", lambda e: e.matmul(o, l, r, start=start, stop=stop), r=_keys(lhsT, rhs), w=out.keys)

    def tr(self, out, in_, ident):
        o, i, d = out.ap, in_.ap, ident.ap
        self.P.add("pe", lambda e: e.transpose(o, i, d), r=_keys(in_, ident), w=out.keys)

    def act(self, out, in_, func, bias=None, scale=1.0, eng="act"):
        o, i = out.ap, in_.ap
        b, s = _ap(bias), _ap(scale)
        kw = {}
        if b is not None:
            kw["bias"] = b
        self.P.add(eng, lambda e: e.activation(out=o, in_=i, func=func, scale=s, **kw),
                   r=_keys(in_, bias, scale), w=out.keys)

    def ts(self, out, in0, s1, op0, s2=None, op1=None, eng="dve"):
        o, i = out.ap, in0.ap
        a, b = _ap(s1), _ap(s2)
        if op1 is None:
            fn = lambda e: e.tensor_scalar(out=o, in0=i, scalar1=a, scalar2=None, op0=op0)
        else:
            fn = lambda e: e.tensor_scalar(out=o, in0=i, scalar1=a, scalar2=b, op0=op0, op1=op1)
        self.P.add(eng, fn, r=_keys(in0, s1, s2), w=out.keys)

    def tt(self, out, in0, in1, op, eng="dve"):
        o, a, b = out.ap, in0.ap, in1.ap
        self.P.add(eng, lambda e: e.tensor_tensor(out=o, in0=a, in1=b, op=op), r=_keys(in0, in1), w=out.keys)

    def stt(self, out, in0, scalar, in1, op0, op1):
        o, a, b = out.ap, in0.ap, in1.ap
        s = _ap(scalar)
        self.P.add("dve", lambda e: e.scalar_tensor_tensor(out=o, in0=a, scalar=s, in1=b, op0=op0, op1=op1),
                   r=_keys(in0, scalar, in1), w=out.keys)

    def cp(self, out, in_, eng="dve"):
        o, i = out.ap, in_.ap
        if eng == "act":
            self.P.add("act", lambda e: e.activation(out=o, in_=i, func=AF.Copy), r=in_.keys, w=out.keys)
        else:
            self.P.add(eng, lambda e: e.tensor_copy(out=o, in_=i), r=in_.keys, w=out.keys)

    def scan(self, out, d0, d1):
        o, a, b = out.ap, d0.ap, d1.ap
        self.P.add("dve", lambda e: e.tensor_tensor_scan(out=o, data0=a, data1=b, initial=0.0,
                                                         op0=ALU.mult, op1=ALU.add),
                   r=_keys(d0, d1), w=out.keys)

    def memset(self, out, val):
        o = out.ap
        self.P.add("dve", lambda e: e.memset(o, val), w=out.keys)

    def max8(self, out, in_):
        o, i = out.ap, in_.ap
        self.P.add("dve", lambda e: e.max(out=o, in_=i), r=in_.keys, w=out.keys)

    def rsum(self, out, in_):
        o, i = out.ap, in_.ap
        self.P.add("dve", lambda e: e.reduce_sum(out=o, in_=i, axis=AX.X), r=in_.keys, w=out.keys)

    def recip(self, out, in_):
        o, i = out.ap, in_.ap
        self.P.add("dve", lambda e: e.reciprocal(out=o, in_=i), r=in_.keys, w=out.keys)

    def dma(self, out, in_, q, grp, rk=(), wk=(), is_out=False):
        o, i = _ap(out), _ap(in_)
        self.P.add(q, lambda e: e.dma_start(out=o, in_=i), r=list(rk) + _keys(in_), w=list(wk) + _keys(out),
                   dma=True, grp=grp, is_out=is_out)


PV_FIELDS = [("b_mod", 96), ("conv_a_w", 32), ("conv_a_b", 8), ("rg_br", 16), ("rg_bi", 16), ("rg_lambda", 16),
             ("conv_b_w", 72), ("conv_b_b", 24), ("hy_bias", 8), ("b_gate", 32), ("b_out", 16),
             ("ln1_g", 16), ("ln1_b", 16), ("ln2_g", 16), ("ln2_b", 16), ("router_b", 32), ("hy3", 3),
             ("b_gu", 1024)]
PV_OFF = {}
_o = 0
for _n, _c in PV_FIELDS:
    PV_OFF[_n] = _o
    _o += _c
NPV = _o
MISC_FLAG, MISC_H0, MISC_NTN, MISC_NS, MISC_NNS, NMISC = 0, 1, 33, 41, 49, 57


def _fm(v):
    return np.ascontiguousarray(v.reshape(-1, 128).T)


def pack_pvec(inp, l):
    pv = np.zeros((128, NPV), np.float32)

    def put(name, arr):
        pv[:, PV_OFF[name]:PV_OFF[name] + arr.shape[1]] = arr
    put("b_mod", _fm(inp["b_mod"][l]))
    put("conv_a_w", inp["conv_a_w"][l].reshape(4, 8, 128).transpose(2, 1, 0).reshape(128, 32))
    put("conv_a_b", _fm(inp["conv_a_b"][l]))
    put("rg_br", _fm(inp["rg_br"][l].reshape(-1)))
    put("rg_bi", _fm(inp["rg_bi"][l].reshape(-1)))
    put("rg_lambda", _fm(inp["rg_lambda"][l].reshape(-1)))
    put("conv_b_w", inp["conv_b_w"][l].reshape(3, 24, 128).transpose(2, 1, 0).reshape(128, 72))
    put("conv_b_b", _fm(inp["conv_b_b"][l]))
    put("hy_bias", _fm(inp["hy_bias"][l]))
    put("b_gate", _fm(inp["b_gate"][l]))
    put("b_out", _fm(inp["b_out"][l]))
    for n in ("ln1_g", "ln1_b", "ln2_g", "ln2_b"):
        put(n, _fm(inp[n][l]))
    put("router_b", np.tile(inp["router_b"][l][None, :], (128, 1)))
    h3 = np.zeros((128, 3), np.float32)
    h3[:64, 0] = inp["hy_b1"][l]
    h3[:64, 1] = inp["hy_freq"][l]
    h3[:64, 2] = inp["hy_b2"][l]
    put("hy3", h3)
    put("b_gu", inp["b_gu"][l].reshape(32, 32, 128).transpose(2, 0, 1).reshape(128, 1024))
    return pv


def pack_rgw(inp, l):
    out = np.zeros((128, 32, 128), np.float32)
    for d in range(2):
        for g, nm in enumerate(("rg_wr", "rg_wi")):
            W = inp[nm][l, d]
            for j in range(8):
                q = (d * 2 + g) * 8 + j
                out[0:64, q, 0:64] = W[2 * j]
                out[64:128, q, 64:128] = W[2 * j + 1]
    return out


def seg_tables(Ls):
    t = np.arange(T)
    tau = (t % Ls).astype(np.float64)
    tn = tau / (Ls - 1)
    bands = np.linspace(1e-4, 15, 16)
    ang = (2.0 * math.pi / Ls) * tau[:, None] * bands[None, :]
    z = np.concatenate([tn[:, None], np.cos(ang), -np.sin(ang)], axis=1)
    zT = np.ascontiguousarray(z.T).astype(np.float32)
    ns = (tau != 0).astype(np.float32)
    N = 2 * Ls
    seg = t // Ls
    same = (seg[:, None] == seg[None, :])
    kap = tau
    th = 2.0 * math.pi * (kap[None, :] + 0.5) * tau[:, None] / N
    FC = np.where(same, np.cos(th), 0.0)
    FS = np.where(same, np.sin(th), 0.0)
    IC = np.where(same, np.cos(th.T), 0.0) / Ls
    IS = np.where(same, np.sin(th.T), 0.0) / Ls
    dft = np.stack([FC, FS, IC, IS]).astype(np.float32)
    return zT, tn.astype(np.float32), ns, dft


def grid_pos():
    rows = T // 64
    row = np.repeat(np.arange(rows, dtype=np.float32), 64)
    col = np.tile(np.arange(64, dtype=np.float32), rows)
    q = D // 4
    omega = (1.0 / (10000.0 ** (np.arange(q, dtype=np.float32) / q))).astype(np.float32)
    er = row[:, None] * omega
    ec = col[:, None] * omega
    return np.concatenate([np.sin(er), np.cos(er), np.sin(ec), np.cos(ec)], axis=-1).astype(np.float32)


def build(n_layers=NL, do_mixer=True, do_moe=True, n_exp=NE):
    nc = bass.Bass("TRN2", target_bir_lowering=False)

    def din(name, shape):
        return nc.dram_tensor(name, list(shape), F32, kind="ExternalInput").ap()
    xin = din("xin", [T, D]); pos = din("pos", [T, D]); cond = din("cond", [128, 16])
    misc = din("misc", [128, NMISC]); zT_d = din("zT", [33, T]); dft = din("dft", [4, T, T])
    ident_d = din("ident", [128, 128])
    pvec = din("pvec", [NL, 128, NPV]); rgw = din("rgw", [NL, 128, 32 * 128])
    w_mod = din("w_mod", [NL, D, 6 * D]); w_in = din("w_in", [NL, D, 5120])
    hy_w1 = din("hy_w1", [NL, 33, 64]); hy_w2 = din("hy_w2", [NL, 64, 64]); hy_w3 = din("hy_w3", [NL, 64, 2048])
    hy_b3 = din("hy_b3", [NL, 2048]); hy_decay = din("hy_decay", [NL, 2048])
    w_proj_a = din("w_proj_a", [NL, 1024, D]); w_proj_b = din("w_proj_b", [NL, 1024, D])
    w_gate = din("w_gate", [NL, D, 2 * D]); w_out = din("w_out", [NL, D, D])
    router_w = din("router_w", [NL, D, NE]); w_gu = din("w_gu", [NL, NE, D, 4096])
    w_down = din("w_down", [NL, NE, D, D]); b_down = din("b_down", [NL, NE, D])
    yout = nc.dram_tensor("yout", [T, D], F32, kind="ExternalOutput").ap()
    sout = nc.dram_tensor("sout", [128, NL * 2 * 8 * 4], F32, kind="ExternalOutput").ap()
    xsp = nc.dram_tensor("xsp", [128, 16 * T], F32, kind="Internal").ap()

    with ExitStack() as es:
        def sb(name, shape, dt):
            return es.enter_context(nc.sbuf_tensor(name, list(shape), dt))
        Xt = sb("X", [128, 16 * T], F32)
        HTt = sb("HT", [128, 16 * T], BF16)
        WBt = sb("WB", [128, 3 * 8192], BF16)
        SCt = sb("SC", [128, 12288], F32)
        PVt = sb("PV", [128, NPV], F32)
        CNt = sb("CN", [128, 1024], F32)
        IDt = sb("IDf", [128, 128], F32)
        IBt = sb("IDb", [128, 128], BF16)
        ONt = sb("ONf", [128, 128], F32)
        RWt = sb("RW", [128, 16 * 32], BF16)
        PSt = es.enter_context(nc.psum_tensor("PS", [128, 4096], F32))

        P = Prog(nc, es)
        k = K(P)
        X = Region("X", Xt[:], F32, 512)
        HT = Region("HT", HTt[:], BF16, 512)
        SC = Region("SC", SCt[:], F32, 256)
        CN = Region("CN", CNt[:], F32, 16)
        PS = Region("PS", PSt[:], F32, 512)
        PV = V(PVt[:], ["PV"])
        IDf = V(IDt[:], ["IDf"]); IDb = V(IBt[:], ["IDb"]); ONf = V(ONt[:], ["ONf"])
        RW = V(RWt[:].rearrange("p (k n) -> p k n", k=16), ["RW"])

        def pv(name, c, n=1):
            o = PV_OFF[name] + c
            return V(PVt[:, o:o + n], ["PV"])

        def wslot(i):
            return V(WBt[:, i * 8192:(i + 1) * 8192], [f"WB{i}"])
        wb_ctr = [0]

        def next_slot():
            s = wb_ctr[0] % 3
            wb_ctr[0] += 1
            return s, wslot(s)

        ps_ctr = [0]

        def psb(n=1):
            if n == 2 and ps_ctr[0] % 2 == 1:
                ps_ctr[0] += 1
            b = ps_ctr[0] % 8
            ps_ctr[0] += n
            return PS.view(b * 512, n * 512)

        def xv(c, th=None):
            if th is None:
                return X.view(c * 1024, 1024)
            return X.view(c * 1024 + th * 512, 512)

        def htv(c, th=None):
            if th is None:
                return HT.view(c * 1024, 1024)
            return HT.view(c * 1024 + th * 512, 512)

        cn_off = [0]

        def cn(n):
            o = cn_off[0]
            cn_off[0] += ((n + 15) // 16) * 16
            assert cn_off[0] <= 1024
            return CN.view(o, n)
        MISC = cn(NMISC); CONDv = cn(16); SCONDf = cn(16); SCONDb_f = cn(16)
        MOD = cn(96); SC1P = cn(16); SC2P = cn(16); G1B = cn(16)
        C8 = cn(16); WFA = cn(32); WFB = cn(72); BU1 = None
        HYS = cn(4)
        SOUTv = cn(NL * 64)
        GTOK = cn(256)
        SMALL = cn(64)
        SCONDb = V(SCONDb_f.ap.bitcast(BF16)[:, 0:16], SCONDb_f.keys)

        def misc_c(c, n=1):
            return V(MISC.ap[:, c:c + n], MISC.keys)

        k.dma(IDf, ident_d[:, :], "sp", "c_id")
        k.dma(IDb, ident_d[:, :], "pool", "c_idb")
        k.memset(ONf, 1.0)
        k.dma(MISC, misc[:, :], "sp", "c_misc")
        k.dma(CONDv, cond[:, :], "sp", "c_cond")
        k.memset(SOUTv, 0.0)

        for tt in range(8):
            sx = SC.view((tt % 2) * 4096, 2048)
            spz = SC.view((tt % 2) * 4096 + 2048, 2048)
            k.dma(sx, xin[tt * 128:(tt + 1) * 128, :], "sp", f"ldx{tt % 2}")
            k.dma(spz, pos[tt * 128:(tt + 1) * 128, :], "sp", f"ldp{tt % 2}")
            k.tt(sx, sx, spz, ALU.add)
            for cb in range(4):
                pb = psb()
                for ci in range(4):
                    c = cb * 4 + ci
                    k.tr(pb.sub(pb.ap[:, ci * 128:(ci + 1) * 128]), sx.sub(sx.ap[:, c * 128:(c + 1) * 128]), IDf)
                dst = V(Xt[:].rearrange("p (c t) -> p c t", c=16)[:, cb * 4:(cb + 1) * 4, tt * 128:(tt + 1) * 128],
                        [f"X.{(cb * 4 + ci) * 2 + tt // 4}" for ci in range(4)])
                k.cp(dst, pb.sub(pb.ap.rearrange("p (c t) -> p c t", c=4)), eng="act")

        k.act(SCONDf, CONDv, AF.Sigmoid)
        k.tt(SCONDb, SCONDf, CONDv, ALU.mult)

        def layer_norm(gname, bname):
            for th in range(2):
                s1 = psb(); s2 = psb()
                sq = [SC.view(11264 + i * 512, 512) for i in range(2)]
                for c in range(16):
                    k.mm(s1, ONf, xv(c, th), c == 0, c == 15)
                for c in range(16):
                    q = sq[c % 2]
                    k.act(q, xv(c, th), AF.Square)
                    k.mm(s2, ONf, q, c == 0, c == 15)
                mean = SC.view(10240, 512); rstd = SC.view(10752, 512)
                k.ts(mean, s1, 1.0 / D, ALU.mult)
                k.tt(sq[0], mean, mean, ALU.mult)
                k.stt(rstd, s2, 1.0 / D, sq[0], ALU.mult, ALU.subtract)
                k.ts(rstd, rstd, 0.0, ALU.max, EPS, ALU.add)
                k.act(rstd, rstd, AF.Sqrt)
                k.recip(rstd, rstd)
                for c in range(16):
                    x = xv(c, th)
                    k.tt(x, x, mean, ALU.subtract)
                    k.tt(x, x, rstd, ALU.mult)
                    k.ts(x, x, pv(gname, c), ALU.mult, pv(bname, c), ALU.add)

        def modulate(scp, sh):
            for c in range(16):
                for th in range(2):
                    k.ts(htv(c, th), xv(c, th), V(scp.ap[:, c:c + 1], scp.keys), ALU.mult,
                         V(sh.ap[:, c:c + 1], sh.keys), ALU.add)

        def load_cols(slot, src2d, col_lists, ncol_each, q="pool", grp=None):
            tot = ncol_each * len(col_lists)
            sv = slot.ap[:, 0:16 * tot].rearrange("p (k n) -> p k n", k=16)
            srcv = src2d.rearrange("(k p) n -> p k n", p=128)
            for i, c0 in enumerate(col_lists):
                k.dma(V(sv[:, :, i * ncol_each:(i + 1) * ncol_each], slot.keys), srcv[:, :, c0:c0 + ncol_each],
                      q, grp or ("w" + slot.keys[0]))
            return V(sv, slot.keys)

        def conv_dw(dst, ps, wname, bname, q, ntap, left, wf):
            wcol = lambda t: pv(wname, q * ntap + t)
            k.ts(dst, ps, wcol(left), ALU.mult, pv(bname, q), ALU.add)
            d4 = dst.ap.rearrange("p (s t) -> p s t", s=4)
            p4 = ps.ap.rearrange("p (s t) -> p s t", s=4)
            for t in range(ntap):
                off = t - left
                if off == 0:
                    continue
                wfc = V(wf.ap[:, q * ntap + t:q * ntap + t + 1], wf.keys)
                if off < 0:
                    a = -off
                    k.stt(dst.sub(d4[:, :, a:256]), ps.sub(p4[:, :, 0:256 - a]), wcol(t), dst.sub(d4[:, :, a:256]),
                          ALU.mult, ALU.add)
                    k.stt(dst.sub(d4[:, 1:4, 0:a]), ps.sub(p4[:, 0:3, 256 - a:256]), wfc, dst.sub(d4[:, 1:4, 0:a]),
                          ALU.mult, ALU.add)
                else:
                    a = off
                    k.stt(dst.sub(d4[:, :, 0:256 - a]), ps.sub(p4[:, :, a:256]), wcol(t), dst.sub(d4[:, :, 0:256 - a]),
                          ALU.mult, ALU.add)
                    k.stt(dst.sub(d4[:, 0:3, 256 - a:256]), ps.sub(p4[:, 1:4, 0:a]), wfc,
                          dst.sub(d4[:, 0:3, 256 - a:256]), ALU.mult, ALU.add)

        def proj_ps(wv, coff):
            ps = psb(2)
            for th in range(2):
                o = ps.sub(ps.ap[:, th * 512:(th + 1) * 512])
                for kk in range(16):
                    k.mm(o, wv.sub(wv.ap[:, kk, coff:coff + 128]), htv(kk, th), kk == 0, kk == 15)
            return ps

        def range_reduce_sin(dst, src, tmp):
            inv = 1.0 / (2.0 * math.pi)
            ti = V(tmp.ap.bitcast(mybir.dt.int32), tmp.keys)
            k.ts(dst, src, inv, ALU.mult, 64.5, ALU.add)
            k.cp(ti, dst)
            k.cp(dst, ti)
            k.ts(dst, dst, -64.0, ALU.add, -2.0 * math.pi, ALU.mult)
            k.tt(dst, dst, src, ALU.add)
            k.ts(tmp, dst, math.pi, ALU.is_gt, -2.0 * math.pi, ALU.mult)
            k.tt(dst, dst, tmp, ALU.add)
            k.ts(tmp, dst, -math.pi, ALU.is_lt, 2.0 * math.pi, ALU.mult)
            k.tt(dst, dst, tmp, ALU.add)
            k.ts(dst, dst, PI_LO, ALU.min, -PI_LO, ALU.max)
            k.act(dst, dst, AF.Sin)

        def mixer(l):
            sh1 = V(MOD.ap[:, 0:16], MOD.keys)
            g1 = V(MOD.ap[:, 32:48], MOD.keys)
            modulate(SC1P, sh1)
            k.dma(xsp[:, :], V(Xt[:], [f"X.{u}" for u in range(32)]), "sp", "spill", wk=["xsp"])
            MG = lambda c, th: X.view(c * 512 + th * 256, 256, BF16)
            YA = lambda j, th=None: X.view(8192 + j * 512 + (0 if th is None else th * 256),
                                           512 if th is None else 256, BF16)
            YB = lambda j, th=None: X.view(12288 + j * 512 + (0 if th is None else th * 256),
                                           512 if th is None else 256, BF16)
            flag = misc_c(MISC_FLAG)
            k.ts(WFA, pv("conv_a_w", 0, 32), flag, ALU.mult)
            k.ts(WFB, pv("conv_b_w", 0, 72), flag, ALU.mult)
            k.act(C8, pv("rg_lambda", 0, 16), AF.Exp, scale=-1.0)
            k.act(C8, C8, AF.Ln, bias=1.0)
            k.ts(C8, C8, -8.0, ALU.mult)

            ZT = SC.view(0, 1024); A1 = SC.view(1024, 1024); H1 = SC.view(2048, 1024); TMP = SC.view(3072, 1024)
            W1 = SC.view(4096, 64); W2 = SC.view(4352, 64)
            k.dma(V(ZT.ap[0:33, :], ZT.keys), zT_d[:, :], "sp", "hz")
            k.dma(V(W1.ap[0:33, :], W1.keys), hy_w1[l, :, :], "sp", "hw1")
            k.dma(V(W2.ap[0:64, :], W2.keys), hy_w2[l, :, :], "sp", "hw2")
            fq = V(PVt[0:64, PV_OFF["hy3"] + 1:PV_OFF["hy3"] + 2], ["PV"])
            k.tt(V(HYS.ap[0:64, 0:1], HYS.keys), V(PVt[0:64, PV_OFF["hy3"]:PV_OFF["hy3"] + 1], ["PV"]), fq, ALU.mult)
            k.tt(V(HYS.ap[0:64, 1:2], HYS.keys), V(PVt[0:64, PV_OFF["hy3"] + 2:PV_OFF["hy3"] + 3], ["PV"]), fq, ALU.mult)
            r64 = lambda v: V(v.ap[0:64, :], v.keys)
            p1 = psb(2)
            for th in range(2):
                k.mm(V(p1.ap[0:64, th * 512:(th + 1) * 512], p1.keys), V(W1.ap[0:33, :], W1.keys),
                     V(ZT.ap[0:33, th * 512:(th + 1) * 512], ZT.keys), True, True)
            k.ts(r64(A1), r64(p1), fq, ALU.mult, V(HYS.ap[0:64, 0:1], HYS.keys), ALU.add)
            range_reduce_sin(r64(H1), r64(A1), r64(TMP))
            p2 = psb(2)
            for th in range(2):
                k.mm(V(p2.ap[0:64, th * 512:(th + 1) * 512], p2.keys), V(W2.ap[0:64, :], W2.keys),
                     V(H1.ap[0:64, th * 512:(th + 1) * 512], H1.keys), True, True)
            k.ts(r64(A1), r64(p2), fq, ALU.mult, V(HYS.ap[0:64, 1:2], HYS.keys), ALU.add)
            H2 = SC.view(0, 1024)
            range_reduce_sin(r64(H2), r64(A1), r64(TMP))

            for g in range(4):
                W3 = SC.view(1024, 512); B3 = SC.view(1536, 512); DC = SC.view(2048, 512); DB = SC.view(2560, 512)
                WIN = SC.view(3072, 512); FT = SC.view(3584, 512)
                HS = SC.view(4608, 1024, BF16); HD = SC.view(5632, 1024, BF16)
                for hlf in range(2):
                    c0 = hlf * 1024 + g * 256
                    k.dma(V(W3.ap[0:64, hlf * 256:(hlf + 1) * 256], W3.keys), hy_w3[l, :, c0:c0 + 256], "sp", "hw3")
                    k.dma(V(B3.ap[0:1, hlf * 256:(hlf + 1) * 256], B3.keys), hy_b3[l:l + 1, c0:c0 + 256], "sp", "hb3")
                    k.dma(V(DC.ap[0:1, hlf * 256:(hlf + 1) * 256], DC.keys), hy_decay[l:l + 1, c0:c0 + 256], "sp", "hdc")
                k.act(V(DC.ap[0:1, :], DC.keys), V(DC.ap[0:1, :], DC.keys), AF.Abs)
                pd = psb()
                k.mm(pd, V(ONf.ap[0:1, :], ONf.keys), V(DC.ap[0:1, :], DC.keys), True, True)
                k.cp(DB, pd, eng="act")
                HS3 = HS.ap.rearrange("p (s c) -> p s c", s=8); HD3 = HD.ap.rearrange("p (s c) -> p s c", s=8)
                for tc in range(8):
                    pf = psb()
                    k.mm(pf, V(H2.ap[0:64, tc * 128:(tc + 1) * 128], H2.keys), V(W3.ap[0:64, :], W3.keys), True, False)
                    k.mm(pf, V(ONf.ap[0:1, :], ONf.keys), V(B3.ap[0:1, :], B3.keys), False, True)
                    k.act(WIN, DB, AF.Exp, scale=misc_c(MISC_NTN + tc))
                    k.tt(FT, pf, WIN, ALU.mult)
                    k.stt(HS.sub(HS3[:, tc, :]), FT.sub(FT.ap[:, 256:512]), misc_c(MISC_NS + tc),
                          FT.sub(FT.ap[:, 0:256]), ALU.mult, ALU.add)
                    k.stt(HD.sub(HD3[:, tc, :]), FT.sub(FT.ap[:, 256:512]), misc_c(MISC_NNS + tc),
                          FT.sub(FT.ap[:, 0:256]), ALU.mult, ALU.add)
                UT = SC.view(6656, 1024, BF16)
                UF = [SC.view(7680 + cc * 512, 512, BF16) for cc in range(2)]
                X0 = [SC.view(8704 + cc * 512, 512, BF16) for cc in range(2)]
                TA = SC.view(1024, 1024); TB = SC.view(2048, 1024)
                s, slot = next_slot()
                wv = load_cols(slot, w_in[l], [2048 + g * 256, 4096 + g * 256], 256)
                s2, slot2 = next_slot()
                wv2 = load_cols(slot2, w_in[l], [3072 + g * 256], 256)
                UT3 = UT.ap.rearrange("p (s c) -> p s c", s=8)
                for cc in range(2):
                    ch = g * 2 + cc
                    pv_ = proj_ps(wv, cc * 128)
                    conv_dw(TA, pv_, "conv_b_w", "conv_b_b", ch, 3, 1, WFB)
                    px1 = proj_ps(wv, 256 + cc * 128)
                    conv_dw(TB, px1, "conv_b_w", "conv_b_b", 16 + ch, 3, 1, WFB)
                    k.tt(UF[cc], TA, TB, ALU.mult)
                    px0 = proj_ps(wv2, cc * 128)
                    conv_dw(TA, px0, "conv_b_w", "conv_b_b", 8 + ch, 3, 1, WFB)
                    k.cp(X0[cc], TA, eng="act")
                    pt = psb()
                    ptb = V(pt.ap.bitcast(BF16), pt.keys)
                    for tc in range(8):
                        k.tr(ptb.sub(ptb.ap[:, tc * 128:(tc + 1) * 128]), UF[cc].sub(UF[cc].ap[:, tc * 128:(tc + 1) * 128]), IDb)
                    k.cp(UT.sub(UT3[:, :, cc * 128:(cc + 1) * 128]), ptb.sub(ptb.ap.rearrange("p (s c) -> p s c", s=8)),
                         eng="act")
                YR = SC.view(9728, 1024, BF16); YI = SC.view(10752, 1024, BF16)
                GC = SC.view(1024, 512); T1 = SC.view(1536, 256); T2 = SC.view(1792, 256)
                sC, slC = next_slot(); sS, slS = next_slot()
                FCv = V(slC.ap.rearrange("p (s n) -> p s n", s=8), slC.keys)
                FSv = V(slS.ap.rearrange("p (s n) -> p s n", s=8), slS.keys)
                k.dma(FCv, dft[0].rearrange("(s p) n -> p s n", p=128), "pool", "w" + slC.keys[0])
                k.dma(FSv, dft[1].rearrange("(s p) n -> p s n", p=128), "pool", "w" + slS.keys[0])
                YR3 = YR.ap.rearrange("p (s c) -> p s c", s=8); YI3 = YI.ap.rearrange("p (s c) -> p s c", s=8)
                for kc in range(8):
                    pg = psb(); pu = psb()
                    for (pp, o, M, src3, srcv) in ((pg, 0, FCv, HS3, HS), (pg, 256, FSv, HD3, HD),
                                                   (pu, 0, FCv, UT3, UT), (pu, 256, FSv, UT3, UT)):
                        for sc_ in range(8):
                            k.mm(pp.sub(pp.ap[:, o:o + 256]), M.sub(M.ap[:, sc_, kc * 128:(kc + 1) * 128]),
                                 srcv.sub(src3[:, sc_, :]), sc_ == 0, sc_ == 7)
                    k.cp(GC, pg, eng="act")
                    gr = GC.sub(GC.ap[:, 0:256]); gi_ = GC.sub(GC.ap[:, 256:512])
                    ur = pu.sub(pu.ap[:, 0:256]); ui = pu.sub(pu.ap[:, 256:512])
                    k.tt(T1, gr, ur, ALU.mult); k.tt(T2, gi_, ui, ALU.mult)
                    k.tt(YR.sub(YR3[:, kc, :]), T1, T2, ALU.subtract)
                    k.tt(T1, gr, ui, ALU.mult); k.tt(T2, gi_, ur, ALU.mult)
                    k.tt(YI.sub(YI3[:, kc, :]), T1, T2, ALU.add)
                sC, slC = next_slot(); sS, slS = next_slot()
                ICv = V(slC.ap.rearrange("p (s n) -> p s n", s=8), slC.keys)
                ISv = V(slS.ap.rearrange("p (s n) -> p s n", s=8), slS.keys)
                k.dma(ICv, dft[2].rearrange("(s p) n -> p s n", p=128), "pool", "w" + slC.keys[0])
                k.dma(ISv, dft[3].rearrange("(s p) n -> p s n", p=128), "pool", "w" + slS.keys[0])
                for cc in range(2):
                    ch = g * 2 + cc
                    for th in range(2):
                        py = psb()
                        for kc in range(8):
                            k.mm(py, YR.sub(YR3[:, kc, cc * 128:(cc + 1) * 128]), ICv.sub(ICv.ap[:, kc, th * 512:(th + 1) * 512]),
                                 kc == 0, False)
                        for kc in range(8):
                            k.mm(py, YI.sub(YI3[:, kc, cc * 128:(cc + 1) * 128]), ISv.sub(ISv.ap[:, kc, th * 512:(th + 1) * 512]),
                                 False, kc == 7)
                        t_ = SC.view(1536 + 0, 512)
                        k.stt(t_, UF[cc].sub(UF[cc].ap[:, th * 512:(th + 1) * 512]), pv("hy_bias", ch), py, ALU.mult, ALU.add)
                        k.tt(YB(ch, th), t_, X0[cc].sub(X0[cc].ap[:, th * 512:(th + 1) * 512]), ALU.mult)

            RG = SC.view(0, 2048, BF16)
            RG3 = V(RG.ap.rearrange("p (q n) -> p q n", q=32), RG.keys)
            k.dma(RG3, rgw[l, :, :].rearrange("p (q n) -> p q n", q=32), "pool", "rgw")
            U = SC.view(2048, 1024); UB = SC.view(3072, 512, BF16); GG = SC.view(3584, 1024); TG = SC.view(4608, 1024)
            Rr = SC.view(5632, 1024); Gi = SC.view(6656, 1024); Aa = SC.view(7680, 1024); Bb = SC.view(8704, 1024)
            Hd = [SC.view(9728, 1024), SC.view(10752, 1024)]
            for jp in range(4):
                s, slot = next_slot()
                wv = load_cols(slot, w_in[l], [jp * 256, 1024 + jp * 256], 256)
                for cc in range(2):
                    j = jp * 2 + cc
                    pxa = proj_ps(wv, cc * 128)
                    conv_dw(U, pxa, "conv_a_w", "conv_a_b", j, 4, 2, WFA)
                    k.cp(UB, U, eng="act")
                    pga = proj_ps(wv, 256 + cc * 128)
                    k.cp(GG, pga, eng="act")
                    k.tt(TG, GG, GG, ALU.mult)
                    k.ts(TG, TG, 0.044715, ALU.mult, 1.0, ALU.add)
                    k.tt(TG, TG, GG, ALU.mult)
                    k.act(TG, TG, AF.Sigmoid, scale=1.5957691216057308)
                    k.tt(GG, GG, TG, ALU.mult)
                    for d in range(2):
                        pr = psb(2); pi_ = psb(2)
                        for th in range(2):
                            k.mm(pr.sub(pr.ap[:, th * 512:(th + 1) * 512]), RG3.sub(RG3.ap[:, (d * 2) * 8 + j, :]),
                                 UB.sub(UB.ap[:, th * 512:(th + 1) * 512]), True, True)
                            k.mm(pi_.sub(pi_.ap[:, th * 512:(th + 1) * 512]), RG3.sub(RG3.ap[:, (d * 2 + 1) * 8 + j, :]),
                                 UB.sub(UB.ap[:, th * 512:(th + 1) * 512]), True, True)
                        k.act(Rr, pr, AF.Sigmoid, bias=pv("rg_br", d * 8 + j))
                        k.act(Gi, pi_, AF.Sigmoid, bias=pv("rg_bi", d * 8 + j))
                        k.act(Aa, Rr, AF.Exp, scale=V(C8.ap[:, d * 8 + j:d * 8 + j + 1], C8.keys))
                        k.tt(Bb, Aa, Aa, ALU.mult)
                        k.ts(Bb, Bb, -1.0, ALU.mult, 1.0, ALU.add)
                        k.ts(Bb, Bb, 0.0, ALU.max)
                        k.act(Bb, Bb, AF.Sqrt)
                        k.tt(Gi, Gi, U, ALU.mult)
                        k.tt(Bb, Bb, Gi, ALU.mult)
                        edge = 0 if d == 0 else 1023
                        h0c = misc_c(MISC_H0 + l * 16 + d * 8 + j)
                        k.stt(Bb.sub(Bb.ap[:, edge:edge + 1]), Aa.sub(Aa.ap[:, edge:edge + 1]), h0c,
                              Bb.sub(Bb.ap[:, edge:edge + 1]), ALU.mult, ALU.add)
                        a4 = Aa.ap.rearrange("p (s t) -> p s t", s=4)
                        ecol = a4[:, :, 0:1] if d == 0 else a4[:, :, 255:256]
                        k.ts(Aa.sub(ecol), Aa.sub(ecol), flag, ALU.mult)
                        if d == 0:
                            k.scan(Hd[0], Aa, Bb)
                        else:
                            k.scan(Hd[1].sub(Hd[1].ap[:, ::-1]), Aa.sub(Aa.ap[:, ::-1]), Bb.sub(Bb.ap[:, ::-1]))
                        h4 = Hd[d].ap.rearrange("p (s t) -> p s t", s=4)
                        fin = h4[:, :, 255:256] if d == 0 else h4[:, :, 0:1]
                        so = ((l * 2 + d) * 8 + j) * 4
                        k.cp(V(SOUTv.ap[:, so:so + 4].rearrange("p (s o) -> p s o", o=1), SOUTv.keys), Hd[d].sub(fin))
                    k.tt(Hd[0], Hd[0], Hd[1], ALU.add)
                    k.tt(YA(j), Hd[0], GG, ALU.mult)

            SA = SC.view(0, 512); SB_ = SC.view(512, 512); M1 = SC.view(1024, 512); M2 = SC.view(1536, 512)
            for half in range(2):
                sa, slA = next_slot(); sb_, slB = next_slot()
                PA = V(slA.ap.rearrange("p (j n) -> p j n", j=8), slA.keys)
                PB = V(slB.ap.rearrange("p (j n) -> p j n", j=8), slB.keys)
                k.dma(PA, w_proj_a[l].rearrange("(j p) n -> p j n", p=128)[:, :, half * 1024:(half + 1) * 1024], "pool", "w" + slA.keys[0])
                k.dma(PB, w_proj_b[l].rearrange("(j p) n -> p j n", p=128)[:, :, half * 1024:(half + 1) * 1024], "pool", "w" + slB.keys[0])
                for ip in range(4):
                    gs = 3 - sa - sb_
                    slG = wslot(gs)
                    i0 = half * 8 + ip * 2
                    wg = load_cols(slG, w_gate[l], [i0 * 128, 2048 + i0 * 128], 256)
                    for cc in range(2):
                        i = i0 + cc
                        for th in range(2):
                            pga = psb(); pgb = psb(); ppa = psb(); ppb = psb()
                            for kk in range(16):
                                k.mm(pga, wg.sub(wg.ap[:, kk, cc * 128:(cc + 1) * 128]), htv(kk, th), kk == 0, kk == 15)
                            for kk in range(16):
                                k.mm(pgb, wg.sub(wg.ap[:, kk, 256 + cc * 128:256 + (cc + 1) * 128]), htv(kk, th), kk == 0, kk == 15)
                            il = (i - half * 8) * 128
                            for jj in range(8):
                                k.mm(ppa, PA.sub(PA.ap[:, jj, il:il + 128]), YA(jj, th), jj == 0, jj == 7)
                            for jj in range(8):
                                k.mm(ppb, PB.sub(PB.ap[:, jj, il:il + 128]), YB(jj, th), jj == 0, jj == 7)
                            k.act(SA, pga, AF.Sigmoid, bias=pv("b_gate", i))
                            k.act(SB_, pgb, AF.Sigmoid, bias=pv("b_gate", 16 + i))
                            k.tt(M1, SA, ppa, ALU.mult)
                            k.tt(M2, SB_, ppb, ALU.mult)
                            k.tt(MG(i, th), M1, M2, ALU.add)
                wb_ctr[0] = 0
            for c in range(16):
                for th in range(2):
                    k.cp(htv(c, th), MG(c, th), eng="act")
            k.dma(V(Xt[:], [f"X.{u}" for u in range(32)]), xsp[:, :], "sp", "spill", rk=["xsp"])
            for cb in range(4):
                s, slot = next_slot()
                wv = load_cols(slot, w_out[l], [cb * 512], 512)
                for ci in range(4):
                    c = cb * 4 + ci
                    for th in range(2):
                        po = psb()
                        for kk in range(16):
                            k.mm(po, wv.sub(wv.ap[:, kk, ci * 128:(ci + 1) * 128]), htv(kk, th), kk == 0, kk == 15)
                        x = xv(c, th)
                        k.ts(x, x, ALPHA, ALU.mult, V(G1B.ap[:, c:c + 1], G1B.keys), ALU.add)
                        k.stt(x, po, V(g1.ap[:, c:c + 1], g1.keys), x, ALU.mult, ALU.add)
            layer_norm("ln1_g", "ln1_b")

        def moe(l):
            sh2 = V(MOD.ap[:, 48:64], MOD.keys)
            g2 = V(MOD.ap[:, 80:96], MOD.keys)
            modulate(SC2P, sh2)
            k.dma(RW, router_w[l].rearrange("(k p) n -> p k n", p=128), "pool", "rw")
            BD = SC.view(0, 2048)
            GT = SC.view(2048, 1024)
            k.dma(V(BD.ap[0:32, :], BD.keys), b_down[l, :, :], "sp", "bd")
            LG = V(SMALL.ap[:, 0:32], SMALL.keys); M8 = V(SMALL.ap[:, 32:40], SMALL.keys)
            NM = V(SMALL.ap[:, 40:41], SMALL.keys); SM = V(SMALL.ap[:, 41:42], SMALL.keys)
            EX = SC.view(3072, 32); MK = SC.view(3328, 32)
            G3 = GTOK.ap.rearrange("p (t e) -> p t e", t=8)
            for tt in range(8):
                pl = psb()
                for kk in range(16):
                    k.mm(pl.sub(pl.ap[:, 0:32]), htv(kk).sub(htv(kk).ap[:, tt * 128:(tt + 1) * 128]), RW.sub(RW.ap[:, kk, :]),
                         kk == 0, kk == 15)
                k.tt(LG, pl.sub(pl.ap[:, 0:32]), pv("router_b", 0, 32), ALU.add)
                k.max8(M8, LG)
                k.ts(MK, LG, V(M8.ap[:, 3:4], M8.keys), ALU.is_ge)
                k.ts(NM, V(M8.ap[:, 0:1], M8.keys), -1.0, ALU.mult)
                k.act(EX, LG, AF.Exp, bias=NM)
                k.tt(EX, EX, MK, ALU.mult)
                k.rsum(SM, EX)
                k.recip(SM, SM)
                k.ts(GTOK.sub(G3[:, tt, :]), EX, SM, ALU.mult)
                pT = psb()
                k.tr(V(pT.ap[0:32, 0:128], pT.keys), GTOK.sub(G3[:, tt, :]), IDf)
                k.cp(V(GT.ap[0:32, tt * 128:(tt + 1) * 128], GT.keys), V(pT.ap[0:32, 0:128], pT.keys), eng="act")
            for c in range(16):
                for th in range(2):
                    pb_ = psb()
                    k.mm(pb_, V(BD.ap[0:32, c * 128:(c + 1) * 128], BD.keys), V(GT.ap[0:32, th * 512:(th + 1) * 512], GT.keys),
                         True, True)
                    x = xv(c, th)
                    k.ts(x, x, ALPHA, ALU.mult)
                    k.stt(x, pb_, V(g2.ap[:, c:c + 1], g2.keys), x, ALU.mult, ALU.add)
            ACTT = lambda m, th: SC.view(m * 512 + th * 256, 256, BF16)
            GB = SC.view(4096, 1024)
            DG = [SC.view(5120, 128), SC.view(5248, 128)]
            TMPS = [[SC.view(5632 + s_ * 1536 + i * 512, 512) for i in range(3)] for s_ in range(2)]
            BU1 = SC.view(8704, 1024)
            k.ts(BU1, pv("b_gu", 0, 1024), 1.0, ALU.add)
            blk = 0
            for e in range(n_exp):
                pgb_ = psb(2)
                for tt in range(8):
                    dg = DG[tt % 2]
                    k.ts(dg, IDf, V(G3[:, tt, e:e + 1], GTOK.keys), ALU.mult)
                    k.mm(pgb_.sub(pgb_.ap[:, tt * 128:(tt + 1) * 128]), ONf, dg, True, True)
                k.cp(GB, pgb_, eng="act")
                for hf in range(2):
                    for sp_ in range(4):
                        s, slot = next_slot()
                        m0 = hf * 8 + sp_ * 2
                        wv = load_cols(slot, w_gu[l, e], [m0 * 128, 2048 + m0 * 128], 256)
                        for cc in range(2):
                            m = m0 + cc
                            ml = sp_ * 2 + cc
                            for th in range(2):
                                pg = psb(); pu = psb()
                                for kk in range(16):
                                    k.mm(pg, wv.sub(wv.ap[:, kk, cc * 128:(cc + 1) * 128]), htv(kk, th), kk == 0, kk == 15)
                                for kk in range(16):
                                    k.mm(pu, wv.sub(wv.ap[:, kk, 256 + cc * 128:256 + (cc + 1) * 128]), htv(kk, th), kk == 0, kk == 15)
                                tg, tsg, tl = TMPS[blk % 2]
                                blk += 1
                                k.ts(tg, pg, pv("b_gu", e * 32 + m), ALU.add, 7.0, ALU.min)
                                k.act(tsg, tg, AF.Sigmoid, scale=1.702)
                                k.act(tl, pu, AF.Identity, bias=V(BU1.ap[:, e * 32 + 16 + m:e * 32 + 16 + m + 1], BU1.keys))
                                k.ts(tl, tl, -6.0, ALU.max, 8.0, ALU.min)
                                k.tt(tg, tg, tsg, ALU.mult)
                                k.tt(tl, tl, GB.sub(GB.ap[:, th * 512:(th + 1) * 512]), ALU.mult)
                                k.tt(ACTT(ml, th), tg, tl, ALU.mult)
                    dsl = []
                    for sd in range(2):
                        s, slot = next_slot()
                        r0 = (hf * 8 + sd * 4) * 128
                        dv = V(slot.ap.rearrange("p (j a b) -> p j a b", j=4, a=4), slot.keys)
                        k.dma(dv, w_down[l, e, r0:r0 + 512, :].rearrange("(j p) (a b) -> p j a b", p=128, b=512),
                              "pool", "w" + slot.keys[0])
                        dsl.append(V(slot.ap.rearrange("p (j n) -> p j n", j=4), slot.keys))
                    for c in range(16):
                        for th in range(2):
                            po = psb()
                            for ml in range(8):
                                dv = dsl[ml // 4]
                                k.mm(po, dv.sub(dv.ap[:, ml % 4, c * 128:(c + 1) * 128]), ACTT(ml, th), ml == 0, ml == 7)
                            x = xv(c, th)
                            k.stt(x, po, V(g2.ap[:, c:c + 1], g2.keys), x, ALU.mult, ALU.add)
            layer_norm("ln2_g", "ln2_b")

        for l in range(n_layers):
            k.dma(PV, pvec[l, :, :], "sp", "pv")
            mod_ps = psb()
            for j in range(24):
                s, slot = next_slot()
                wv = load_cols(slot, w_mod[l], [j * 512], 512)
                for cc in range(4):
                    col = j * 4 + cc
                    for kk in range(16):
                        k.mm(mod_ps.sub(mod_ps.ap[:, col:col + 1]), wv.sub(wv.ap[:, kk, cc * 128:(cc + 1) * 128]),
                             V(SCONDb.ap[:, kk:kk + 1], SCONDb.keys), kk == 0, kk == 15)
            k.tt(MOD, mod_ps.sub(mod_ps.ap[:, 0:96]), pv("b_mod", 0, 96), ALU.add)
            mcol = lambda n: V(MOD.ap[:, n * 16:(n + 1) * 16], MOD.keys)
            k.ts(SC1P, mcol(1), 1.0, ALU.add)
            k.ts(SC2P, mcol(4), 1.0, ALU.add)
            k.tt(G1B, mcol(2), pv("b_out", 0, 16), ALU.mult)
            if do_mixer:
                mixer(l)
            if do_moe:
                moe(l)

        for tt in range(8):
            ot = SC.view((tt % 2) * 2048, 2048)
            for cb in range(4):
                pb = psb()
                for ci in range(4):
                    c = cb * 4 + ci
                    xc = xv(c, tt // 4)
                    k.tr(pb.sub(pb.ap[:, ci * 128:(ci + 1) * 128]), xc.sub(xc.ap[:, (tt % 4) * 128:(tt % 4 + 1) * 128]), IDf)
                k.cp(ot.sub(ot.ap[:, cb * 512:(cb + 1) * 512]), pb, eng="act")
            k.dma(yout[tt * 128:(tt + 1) * 128, :], ot, "sp", f"st{tt % 2}", is_out=True)
        k.dma(sout[:, :], SOUTv, "sp", "sts", is_out=True)
        P.emit()
    return nc


_CACHE = {}


def kernel(**inp):
    inp = {k_: np.asarray(v) for k_, v in inp.items()}
    n_layers = _CACHE.get("n_layers", NL)
    key = ("nc", n_layers, _CACHE.get("do_mixer", True), _CACHE.get("do_moe", True), _CACHE.get("n_exp", NE))
    nc = build(n_layers, key[2], key[3], key[4])
    pvec = np.stack([pack_pvec(inp, l) for l in range(NL)])
    rgw = np.stack([pack_rgw(inp, l).reshape(128, 32 * 128) for l in range(NL)])
    ident = np.eye(128, dtype=np.float32)
    gp = grid_pos()
    zeros_pos = np.zeros((T, D), np.float32)
    tabs = {Ls: seg_tables(Ls) for Ls in (1024, 256)}
    shared = dict(ident=ident, pvec=pvec, rgw=rgw)
    for n in ("w_mod", "w_in", "hy_w1", "hy_w2", "hy_w3", "hy_b3", "hy_decay", "w_proj_a", "w_proj_b", "w_gate",
              "w_out", "router_w", "w_gu", "w_down", "b_down"):
        shared[n] = np.ascontiguousarray(inp[n], dtype=np.float32)
    in_maps = []
    for core in range(8):
        sample = core < 4
        Ls = 1024 if sample else 256
        zT, tn, ns, dft = tabs[Ls]
        m = dict(shared)
        misc = np.zeros((128, NMISC), np.float32)
        if sample:
            b = core
            m["xin"] = np.ascontiguousarray(inp["x_sample"][b])
            m["pos"] = gp
            m["cond"] = _fm(inp["c"][b])
            misc[:, MISC_FLAG] = 1.0
            st = inp["state_rglru"][b]
            for l in range(NL):
                for d in range(2):
                    misc[:, MISC_H0 + l * 16 + d * 8:MISC_H0 + l * 16 + d * 8 + 8] = _fm(st[l, d])
        else:
            p0 = (core - 4) * 4
            m["xin"] = np.ascontiguousarray(inp["x_prompt"][p0:p0 + 4].reshape(T, D))
            m["pos"] = zeros_pos
            m["cond"] = _fm(inp["c_ctx"])
        misc[:, MISC_NTN:MISC_NTN + 8] = -_fm(tn)
        misc[:, MISC_NS:MISC_NS + 8] = _fm(ns)
        misc[:, MISC_NNS:MISC_NNS + 8] = -_fm(ns)
        m["misc"] = misc
        m["zT"] = zT
        m["dft"] = dft
        in_maps.append(m)
    res = run_bass_kernel_spmd(nc, in_maps, core_ids=list(range(8)))
    outs = res.results
    y_sample = np.stack([outs[c]["yout"] for c in range(4)]).astype(np.float32)
    y_prompt = np.concatenate([outs[c]["yout"].reshape(4, 256, D) for c in range(4, 8)]).astype(np.float32)
    ns_ = np.zeros((16, NL, 2, 1024), np.float32)
    for c in range(4, 8):
        so = outs[c]["sout"].reshape(128, NL, 2, 8, 4)
        ns_[(c - 4) * 4:(c - 4) * 4 + 4] = so.transpose(4, 1, 2, 3, 0).reshape(4, NL, 2, 1024)
    _CACHE["last"] = outs
    return (y_prompt, y_sample, ns_)
```

```python
import math
from contextlib import ExitStack
import numpy as np
import concourse.bass as bass
import concourse.mybir as mybir
from concourse.bass_utils import run_bass_kernel_spmd

F32 = mybir.dt.float32
BF16 = mybir.dt.bfloat16
AF = mybir.ActivationFunctionType
ALU = mybir.AluOpType
AX = mybir.AxisListType

D = 2048
T = 1024
NL = 2
NE = 32
ALPHA = (2 * NL) ** 0.25
EPS = 1e-5
PI_LO = 3.1415925


class V:
    def __init__(self, ap, keys):
        self.ap = ap
        self.keys = tuple(keys)

    def sub(self, ap):
        return V(ap, self.keys)


class Region:
    def __init__(self, name, ap, dtype, unit):
        self.name, self.ap, self.dtype, self.unit = name, ap, dtype, unit

    def view(self, c0, n, dtype=None):
        keys = [f"{self.name}.{u}" for u in range(c0 // self.unit, (c0 + n - 1) // self.unit + 1)]
        ap = self.ap[:, c0:c0 + n]
        if dtype is not None and dtype != self.dtype:
            ap = ap.bitcast(dtype)
        return V(ap, keys)


class Op:
    __slots__ = ("eng", "fn", "deps", "dma", "grp", "sig", "val", "waits", "sem")


class Prog:
    def __init__(self, nc, es):
        self.nc, self.es = nc, es
        self.ops = []
        self.last_w = {}
        self.readers = {}
        self.grp_sem = {}
        self.eng_sem = {}
        for e in ("pe", "act", "dve", "pool"):
            self.eng_sem[e] = es.enter_context(nc.semaphore("sem_" + e))
        self.out_groups = set()

    def add(self, eng, fn, r=(), w=(), dma=False, grp=None, is_out=False):
        idx = len(self.ops)
        deps = set()
        for k in r:
            d = self.last_w.get(k)
            if d is not None:
                deps.add(d)
        for k in w:
            d = self.last_w.get(k)
            if d is not None:
                deps.add(d)
            rd = self.readers.get(k)
            if rd:
                deps.update(rd.values())
        for k in r:
            rd = self.readers.setdefault(k, {})
            rd[("dma", idx) if dma else eng] = idx
        for k in w:
            self.last_w[k] = idx
            self.readers[k] = {}
        op = Op()
        op.eng, op.fn, op.dma, op.grp, op.sig, op.val, op.waits = eng, fn, dma, grp, dma, 0, None
        fdeps = []
        for d in deps:
            o = self.ops[d]
            if (not o.dma) and (not dma) and o.eng == eng and eng == "pe":
                continue
            if not o.dma:
                o.sig = True
            fdeps.append(d)
        op.deps = fdeps
        if dma:
            if grp not in self.grp_sem:
                self.grp_sem[grp] = self.es.enter_context(self.nc.semaphore("g_" + grp))
            if is_out:
                self.out_groups.add(grp)
        self.ops.append(op)
        return idx

    def emit(self):
        nc = self.nc
        cnt = {e: 0 for e in self.eng_sem}
        gcnt = {g: 0 for g in self.grp_sem}
        waited = {e: {} for e in ("pe", "act", "dve", "pool", "sp")}
        for op in self.ops:
            ws = {}
            for d in op.deps:
                o = self.ops[d]
                if o.dma:
                    key = ("g", o.grp)
                    val = gcnt[o.grp]
                else:
                    key = ("e", o.eng)
                    val = o.val
                if ws.get(key, 0) < val:
                    ws[key] = val
            wl = []
            wd = waited[op.eng]
            for key, val in ws.items():
                if wd.get(key, 0) >= val:
                    continue
                wd[key] = val
                sem = self.grp_sem[key[1]] if key[0] == "g" else self.eng_sem[key[1]]
                wl.append((sem, val))
            op.waits = wl
            if op.dma:
                gcnt[op.grp] += 16
                op.val = gcnt[op.grp]
                op.sem = self.grp_sem[op.grp]
            elif op.sig:
                cnt[op.eng] += 1
                op.val = cnt[op.eng]
                op.sem = self.eng_sem[op.eng]
        final = [(self.grp_sem[g], gcnt[g]) for g in sorted(self.out_groups)]
        with nc.Block() as block:
            decos = {"pe": block.tensor, "act": block.scalar, "dve": block.vector,
                     "pool": block.gpsimd, "sp": block.sync}
            for ename, deco in decos.items():
                ops = [o for o in self.ops if o.eng == ename]

                def body(e, ops=ops, ename=ename):
                    for o in ops:
                        for sem, val in o.waits:
                            e.wait_ge(sem, val)
                        ins = o.fn(e)
                        if o.sig:
                            ins.then_inc(o.sem, 16 if o.dma else 1)
                    if ename == "sp":
                        for sem, val in final:
                            e.wait_ge(sem, val)
                deco(body)


def _keys(*vs):
    ks = []
    for v in vs:
        if isinstance(v, V):
            ks.extend(v.keys)
    return ks


def _ap(v):
    return v.ap if isinstance(v, V) else v


class K:
    def __init__(self, P):
        self.P = P

    def mm(self, out, lhsT, rhs, start, stop):
        o, l, r = out.ap, lhsT.ap, rhs.ap
        self.P.add("pe", lambda e: e.matmul(o, l, r, start=start, stop=stop), r=_keys(lhsT, rhs), w=out.keys)

    def tr(self, out, in_, ident):
        o, i, d = out.ap, in_.ap, ident.ap
        self.P.add("pe", lambda e: e.transpose(o, i, d), r=_keys(in_, ident), w=out.keys)

    def act(self, out, in_, func, bias=None, scale=1.0, eng="act"):
        o, i = out.ap, in_.ap
        b, s = _ap(bias), _ap(scale)
        kw = {}
        if b is not None:
            kw["bias"] = b
        self.P.add(eng, lambda e: e.activation(out=o, in_=i, func=func, scale=s, **kw),
                   r=_keys(in_, bias, scale), w=out.keys)

    def ts(self, out, in0, s1, op0, s2=None, op1=None, eng="dve"):
        o, i = out.ap, in0.ap
        a, b = _ap(s1), _ap(s2)
        if op1 is None:
            fn = lambda e: e.tensor_scalar(out=o, in0=i, scalar1=a, scalar2=None, op0=op0)
        else:
            fn = lambda e: e.tensor_scalar(out=o, in0=i, scalar1=a, scalar2=b, op0=op0, op1=op1)
        self.P.add(eng, fn, r=_keys(in0, s1, s2), w=out.keys)

    def tt(self, out, in0, in1, op, eng="dve"):
        o, a, b = out.ap, in0.ap, in1.ap
        self.P.add(eng, lambda e: e.tensor_tensor(out=o, in0=a, in1=b, op=op), r=_keys(in0, in1), w=out.keys)

    def stt(self, out, in0, scalar, in1, op0, op1):
        o, a, b = out.ap, in0.ap, in1.ap
        s = _ap(scalar)
        self.P.add("dve", lambda e: e.scalar_tensor_tensor(out=o, in0=a, scalar=s, in1=b, op0=op0, op1=op1),
                   r=_keys(in0, scalar, in1), w=out.keys)

    def cp(self, out, in_, eng="dve"):
        o, i = out.ap, in_.ap
        if eng == "act":
            self.P.add("act", lambda e: e.activation(out=o, in_=i, func=AF.Copy), r=in_.keys, w=out.keys)
        else:
            self.P.add(eng, lambda e: e.tensor_copy(out=o, in_=i), r=in_.keys, w=out.keys)

    def scan(self, out, d0, d1):
        o, a, b = out.ap, d0.ap, d1.ap
        self.P.add("dve", lambda e: e.tensor_tensor_scan(out=o, data0=a, data1=b, initial=0.0,
                                                         op0=ALU.mult, op1=ALU.add),
                   r=_keys(d0, d1), w=out.keys)

    def memset(self, out, val):
        o = out.ap
        self.P.add("dve", lambda e: e.memset(o, val), w=out.keys)

    def max8(self, out, in_):
        o, i = out.ap, in_.ap
        self.P.add("dve", lambda e: e.max(out=o, in_=i), r=in_.keys, w=out.keys)

    def rsum(self, out, in_):
        o, i = out.ap, in_.ap
        self.P.add("dve", lambda e: e.reduce_sum(out=o, in_=i, axis=AX.X), r=in_.keys, w=out.keys)

    def recip(self, out, in_):
        o, i = out.ap, in_.ap
        self.P.add("dve", lambda e: e.reciprocal(out=o, in_=i), r=in_.keys, w=out.keys)

    def dma(self, out, in_, q, grp, rk=(), wk=(), is_out=False):
        o, i = _ap(out), _ap(in_)
        self.P.add(q, lambda e: e.dma_start(out=o, in_=i), r=list(rk) + _keys(in_), w=list(wk) + _keys(out),
                   dma=True, grp=grp, is_out=is_out)


PV_FIELDS = [("b_mod", 96), ("conv_a_w", 32), ("conv_a_b", 8), ("rg_br", 16), ("rg_bi", 16), ("rg_lambda", 16),
             ("conv_b_w", 72), ("conv_b_b", 24), ("hy_bias", 8), ("b_gate", 32), ("b_out", 16),
             ("ln1_g", 16), ("ln1_b", 16), ("ln2_g", 16), ("ln2_b", 16), ("router_b", 32), ("hy3", 3),
             ("b_gu", 1024)]
PV_OFF = {}
_o = 0
for _n, _c in PV_FIELDS:
    PV_OFF[_n] = _o
    _o += _c
NPV = _o
MISC_FLAG, MISC_H0, MISC_NTN, MISC_NS, MISC_NNS, NMISC = 0, 1, 33, 41, 49, 57


def _fm(v):
    return np.ascontiguousarray(v.reshape(-1, 128).T)


def pack_pvec(inp, l):
    pv = np.zeros((128, NPV), np.float32)

    def put(name, arr):
        pv[:, PV_OFF[name]:PV_OFF[name] + arr.shape[1]] = arr
    put("b_mod", _fm(inp["b_mod"][l]))
    put("conv_a_w", inp["conv_a_w"][l].reshape(4, 8, 128).transpose(2, 1, 0).reshape(128, 32))
    put("conv_a_b", _fm(inp["conv_a_b"][l]))
    put("rg_br", _fm(inp["rg_br"][l].reshape(-1)))
    put("rg_bi", _fm(inp["rg_bi"][l].reshape(-1)))
    put("rg_lambda", _fm(inp["rg_lambda"][l].reshape(-1)))
    put("conv_b_w", inp["conv_b_w"][l].reshape(3, 24, 128).transpose(2, 1, 0).reshape(128, 72))
    put("conv_b_b", _fm(inp["conv_b_b"][l]))
    put("hy_bias", _fm(inp["hy_bias"][l]))
    put("b_gate", _fm(inp["b_gate"][l]))
    put("b_out", _fm(inp["b_out"][l]))
    for n in ("ln1_g", "ln1_b", "ln2_g", "ln2_b"):
        put(n, _fm(inp[n][l]))
    put("router_b", np.tile(inp["router_b"][l][None, :], (128, 1)))
    h3 = np.zeros((128, 3), np.float32)
    h3[:64, 0] = inp["hy_b1"][l]
    h3[:64, 1] = inp["hy_freq"][l]
    h3[:64, 2] = inp["hy_b2"][l]
    put("hy3", h3)
    put("b_gu", inp["b_gu"][l].reshape(32, 32, 128).transpose(2, 0, 1).reshape(128, 1024))
    return pv


def pack_rgw(inp, l):
    out = np.zeros((128, 32, 128), np.float32)
    for d in range(2):
        for g, nm in enumerate(("rg_wr", "rg_wi")):
            W = inp[nm][l, d]
            for j in range(8):
                q = (d * 2 + g) * 8 + j
                out[0:64, q, 0:64] = W[2 * j]
                out[64:128, q, 64:128] = W[2 * j + 1]
    return out


def seg_tables(Ls):
    t = np.arange(T)
    tau = (t % Ls).astype(np.float64)
    tn = tau / (Ls - 1)
    bands = np.linspace(1e-4, 15, 16)
    ang = (2.0 * math.pi / Ls) * tau[:, None] * bands[None, :]
    z = np.concatenate([tn[:, None], np.cos(ang), -np.sin(ang)], axis=1)
    zT = np.ascontiguousarray(z.T).astype(np.float32)
    ns = (tau != 0).astype(np.float32)
    N = 2 * Ls
    seg = t // Ls
    same = (seg[:, None] == seg[None, :])
    kap = tau
    th = 2.0 * math.pi * (kap[None, :] + 0.5) * tau[:, None] / N
    FC = np.where(same, np.cos(th), 0.0)
    FS = np.where(same, np.sin(th), 0.0)
    IC = np.where(same, np.cos(th.T), 0.0) / Ls
    IS = np.where(same, np.sin(th.T), 0.0) / Ls
    dft = np.stack([FC, FS, IC, IS]).astype(np.float32)
    return zT, tn.astype(np.float32), ns, dft


def grid_pos():
    rows = T // 64
    row = np.repeat(np.arange(rows, dtype=np.float32), 64)
    col = np.tile(np.arange(64, dtype=np.float32), rows)
    q = D // 4
    omega = (1.0 / (10000.0 ** (np.arange(q, dtype=np.float32) / q))).astype(np.float32)
    er = row[:, None] * omega
    ec = col[:, None] * omega
    return np.concatenate([np.sin(er), np.cos(er), np.sin(ec), np.cos(ec)], axis=-1).astype(np.float32)


def build(n_layers=NL, do_mixer=True, do_moe=True, n_exp=NE):
    nc = bass.Bass("TRN2", target_bir_lowering=False)

    def din(name, shape):
        return nc.dram_tensor(name, list(shape), F32, kind="ExternalInput").ap()
    xin = din("xin", [T, D]); pos = din("pos", [T, D]); cond = din("cond", [128, 16])
    misc = din("misc", [128, NMISC]); zT_d = din("zT", [33, T]); dft = din("dft", [4, T, T])
    ident_d = din("ident", [128, 128])
    pvec = din("pvec", [NL, 128, NPV]); rgw = din("rgw", [NL, 128, 32 * 128])
    w_mod = din("w_mod", [NL, D, 6 * D]); w_in = din("w_in", [NL, D, 5120])
    hy_w1 = din("hy_w1", [NL, 33, 64]); hy_w2 = din("hy_w2", [NL, 64, 64]); hy_w3 = din("hy_w3", [NL, 64, 2048])
    hy_b3 = din("hy_b3", [NL, 2048]); hy_decay = din("hy_decay", [NL, 2048])
    w_proj_a = din("w_proj_a", [NL, 1024, D]); w_proj_b = din("w_proj_b", [NL, 1024, D])
    w_gate = din("w_gate", [NL, D, 2 * D]); w_out = din("w_out", [NL, D, D])
    router_w = din("router_w", [NL, D, NE]); w_gu = din("w_gu", [NL, NE, D, 4096])
    w_down = din("w_down", [NL, NE, D, D]); b_down = din("b_down", [NL, NE, D])
    yout = nc.dram_tensor("yout", [T, D], F32, kind="ExternalOutput").ap()
    sout = nc.dram_tensor("sout", [128, NL * 2 * 8 * 4], F32, kind="ExternalOutput").ap()
    xsp = nc.dram_tensor("xsp", [128, 16 * T], F32, kind="Internal").ap()

    with ExitStack() as es:
        def sb(name, shape, dt):
            return es.enter_context(nc.sbuf_tensor(name, list(shape), dt))
        Xt = sb("X", [128, 16 * T], F32)
        HTt = sb("HT", [128, 16 * T], BF16)
        WBt = sb("WB", [128, 3 * 8192], BF16)
        SCt = sb("SC", [128, 12288], F32)
        PVt = sb("PV", [128, NPV], F32)
        CNt = sb("CN", [128, 1280], F32)
        IDt = sb("IDf", [128, 128], F32)
        IBt = sb("IDb", [128, 128], BF16)
        ONt = sb("ONf", [128, 128], F32)
        RWt = sb("RW", [128, 16 * 32], BF16)
        PSt = es.enter_context(nc.psum_tensor("PS", [128, 4096], F32))

        P = Prog(nc, es)
        k = K(P)
        X = Region("X", Xt[:], F32, 512)
        HT = Region("HT", HTt[:], BF16, 512)
        SC = Region("SC", SCt[:], F32, 256)
        CN = Region("CN", CNt[:], F32, 16)
        PS = Region("PS", PSt[:], F32, 512)
        PV = V(PVt[:], ["PV"])
        IDf = V(IDt[:], ["IDf"]); IDb = V(IBt[:], ["IDb"]); ONf = V(ONt[:], ["ONf"])
        RW = V(RWt[:].rearrange("p (k n) -> p k n", k=16), ["RW"])

        def pv(name, c, n=1):
            o = PV_OFF[name] + c
            return V(PVt[:, o:o + n], ["PV"])

        def wslot(i):
            return V(WBt[:, i * 8192:(i + 1) * 8192], [f"WB{i}"])
        wb_ctr = [0]

        def next_slot():
            s = wb_ctr[0] % 3
            wb_ctr[0] += 1
            return s, wslot(s)

        ps_ctr = [0]

        def psb(n=1):
            b = ps_ctr[0] % 7
            if n == 2 and (b % 2 == 1 or b == 6):
                ps_ctr[0] += 1
                b = ps_ctr[0] % 7
                if b == 6:
                    ps_ctr[0] += 1
                    b = 0
            ps_ctr[0] += n
            return PS.view(b * 512, n * 512)

        def xv(c, th=None):
            if th is None:
                return X.view(c * 1024, 1024)
            return X.view(c * 1024 + th * 512, 512)

        def htv(c, th=None):
            if th is None:
                return HT.view(c * 1024, 1024)
            return HT.view(c * 1024 + th * 512, 512)

        cn_off = [0]

        def cn(n):
            o = cn_off[0]
            cn_off[0] += ((n + 15) // 16) * 16
            assert cn_off[0] <= 1280
            return CN.view(o, n)
        MISC = cn(NMISC); CONDv = cn(16); SCONDf = cn(16); SCONDb_f = cn(16)
        MODS = [(cn(96), cn(16), cn(16), cn(96)) for _ in range(2)]
        MOD, SC1P, SC2P = MODS[0][:3]
        G1B = cn(16)
        C8 = cn(16); WFA = cn(32); WFB = cn(72); BU1 = None
        HYS = cn(4)
        SOUTv = cn(NL * 64)
        GTOK = cn(256)
        SMALL = cn(64)
        SCONDb = V(SCONDb_f.ap.bitcast(BF16)[:, 0:16], SCONDb_f.keys)

        def misc_c(c, n=1):
            return V(MISC.ap[:, c:c + n], MISC.keys)

        k.dma(IDf, ident_d[:, :], "sp", "c_id")
        k.dma(IDb, ident_d[:, :], "pool", "c_idb")
        k.memset(ONf, 1.0)
        k.dma(MISC, misc[:, :], "sp", "c_misc")
        k.dma(CONDv, cond[:, :], "sp", "c_cond")
        k.memset(SOUTv, 0.0)

        for tt in range(8):
            sx = SC.view((tt % 2) * 4096, 2048)
            spz = SC.view((tt % 2) * 4096 + 2048, 2048)
            k.dma(sx, xin[tt * 128:(tt + 1) * 128, :], "sp", f"ldx{tt % 2}")
            k.dma(spz, pos[tt * 128:(tt + 1) * 128, :], "sp", f"ldp{tt % 2}")
            k.tt(sx, sx, spz, ALU.add)
            for cb in range(4):
                pb = psb()
                for ci in range(4):
                    c = cb * 4 + ci
                    k.tr(pb.sub(pb.ap[:, ci * 128:(ci + 1) * 128]), sx.sub(sx.ap[:, c * 128:(c + 1) * 128]), IDf)
                dst = V(Xt[:].rearrange("p (c t) -> p c t", c=16)[:, cb * 4:(cb + 1) * 4, tt * 128:(tt + 1) * 128],
                        [f"X.{(cb * 4 + ci) * 2 + tt // 4}" for ci in range(4)])
                k.cp(dst, pb.sub(pb.ap.rearrange("p (c t) -> p c t", c=4)), eng="act")

        k.act(SCONDf, CONDv, AF.Sigmoid)
        k.tt(SCONDb, SCONDf, CONDv, ALU.mult)

        def layer_norm(gname, bname):
            for th in range(2):
                s1 = psb(); s2 = psb()
                sq = [SC.view(11264 + i * 512, 512) for i in range(2)]
                for c in range(16):
                    k.mm(s1, ONf, xv(c, th), c == 0, c == 15)
                for c in range(16):
                    q = sq[c % 2]
                    k.act(q, xv(c, th), AF.Square)
                    k.mm(s2, ONf, q, c == 0, c == 15)
                mean = SC.view(10240, 512); rstd = SC.view(10752, 512)
                k.ts(mean, s1, 1.0 / D, ALU.mult)
                k.tt(sq[0], mean, mean, ALU.mult)
                k.stt(rstd, s2, 1.0 / D, sq[0], ALU.mult, ALU.subtract)
                k.ts(rstd, rstd, 0.0, ALU.max, EPS, ALU.add)
                k.act(rstd, rstd, AF.Sqrt)
                k.recip(rstd, rstd)
                for c in range(16):
                    x = xv(c, th)
                    k.tt(x, x, mean, ALU.subtract)
                    k.tt(x, x, rstd, ALU.mult)
                    k.ts(x, x, pv(gname, c), ALU.mult, pv(bname, c), ALU.add)

        def modulate(scp, sh):
            for c in range(16):
                for th in range(2):
                    k.ts(htv(c, th), xv(c, th), V(scp.ap[:, c:c + 1], scp.keys), ALU.mult,
                         V(sh.ap[:, c:c + 1], sh.keys), ALU.add)

        def load_cols(slot, src2d, col_lists, ncol_each, q="pool", grp=None):
            tot = ncol_each * len(col_lists)
            sv = slot.ap[:, 0:16 * tot].rearrange("p (k n) -> p k n", k=16)
            srcv = src2d.rearrange("(k p) n -> p k n", p=128)
            for i, c0 in enumerate(col_lists):
                k.dma(V(sv[:, :, i * ncol_each:(i + 1) * ncol_each], slot.keys), srcv[:, :, c0:c0 + ncol_each],
                      q, grp or ("w" + slot.keys[0]))
            return V(sv, slot.keys)

        def conv_dw(dst, ps, wname, bname, q, ntap, left, wf):
            wcol = lambda t: pv(wname, q * ntap + t)
            k.ts(dst, ps, wcol(left), ALU.mult, pv(bname, q), ALU.add)
            d4 = dst.ap.rearrange("p (s t) -> p s t", s=4)
            p4 = ps.ap.rearrange("p (s t) -> p s t", s=4)
            for t in range(ntap):
                off = t - left
                if off == 0:
                    continue
                wfc = V(wf.ap[:, q * ntap + t:q * ntap + t + 1], wf.keys)
                if off < 0:
                    a = -off
                    k.stt(dst.sub(d4[:, :, a:256]), ps.sub(p4[:, :, 0:256 - a]), wcol(t), dst.sub(d4[:, :, a:256]),
                          ALU.mult, ALU.add)
                    k.stt(dst.sub(d4[:, 1:4, 0:a]), ps.sub(p4[:, 0:3, 256 - a:256]), wfc, dst.sub(d4[:, 1:4, 0:a]),
                          ALU.mult, ALU.add)
                else:
                    a = off
                    k.stt(dst.sub(d4[:, :, 0:256 - a]), ps.sub(p4[:, :, a:256]), wcol(t), dst.sub(d4[:, :, 0:256 - a]),
                          ALU.mult, ALU.add)
                    k.stt(dst.sub(d4[:, 0:3, 256 - a:256]), ps.sub(p4[:, 1:4, 0:a]), wfc,
                          dst.sub(d4[:, 0:3, 256 - a:256]), ALU.mult, ALU.add)

        def proj_ps(wv, coff):
            ps = psb(2)
            for th in range(2):
                o = ps.sub(ps.ap[:, th * 512:(th + 1) * 512])
                for kk in range(16):
                    k.mm(o, wv.sub(wv.ap[:, kk, coff:coff + 128]), htv(kk, th), kk == 0, kk == 15)
            return ps

        def range_reduce_sin(dst, src, tmp):
            inv = 1.0 / (2.0 * math.pi)
            ti = V(tmp.ap.bitcast(mybir.dt.int32), tmp.keys)
            k.ts(dst, src, inv, ALU.mult, 64.5, ALU.add)
            k.cp(ti, dst)
            k.cp(dst, ti)
            k.ts(dst, dst, -64.0, ALU.add, -2.0 * math.pi, ALU.mult)
            k.tt(dst, dst, src, ALU.add)
            k.ts(tmp, dst, math.pi, ALU.is_gt, -2.0 * math.pi, ALU.mult)
            k.tt(dst, dst, tmp, ALU.add)
            k.ts(tmp, dst, -math.pi, ALU.is_lt, 2.0 * math.pi, ALU.mult)
            k.tt(dst, dst, tmp, ALU.add)
            k.ts(dst, dst, PI_LO, ALU.min, -PI_LO, ALU.max)
            k.act(dst, dst, AF.Sin)

        def mixer(l):
            sh1 = V(MOD.ap[:, 0:16], MOD.keys)
            g1 = V(MOD.ap[:, 32:48], MOD.keys)
            modulate(SC1P, sh1)
            k.dma(xsp[:, :], V(Xt[:], [f"X.{u}" for u in range(32)]), "sp", "spill", wk=["xsp"])
            MG = lambda c, th: X.view(c * 512 + th * 256, 256, BF16)
            YA = lambda j, th=None: X.view(8192 + j * 512 + (0 if th is None else th * 256),
                                           512 if th is None else 256, BF16)
            YB = lambda j, th=None: X.view(12288 + j * 512 + (0 if th is None else th * 256),
                                           512 if th is None else 256, BF16)
            flag = misc_c(MISC_FLAG)
            k.ts(WFA, pv("conv_a_w", 0, 32), flag, ALU.mult)
            k.ts(WFB, pv("conv_b_w", 0, 72), flag, ALU.mult)
            k.act(C8, pv("rg_lambda", 0, 16), AF.Exp, scale=-1.0)
            k.act(C8, C8, AF.Ln, bias=1.0)
            k.ts(C8, C8, -8.0, ALU.mult)

            ZT = SC.view(0, 1024); A1 = SC.view(1024, 1024); H1 = SC.view(2048, 1024); TMP = SC.view(3072, 1024)
            W1 = SC.view(4096, 64); W2 = SC.view(4352, 64)
            k.dma(V(ZT.ap[0:33, :], ZT.keys), zT_d[:, :], "sp", "hz")
            k.dma(V(W1.ap[0:33, :], W1.keys), hy_w1[l, :, :], "sp", "hw1")
            k.dma(V(W2.ap[0:64, :], W2.keys), hy_w2[l, :, :], "sp", "hw2")
            fq = V(PVt[0:64, PV_OFF["hy3"] + 1:PV_OFF["hy3"] + 2], ["PV"])
            k.tt(V(HYS.ap[0:64, 0:1], HYS.keys), V(PVt[0:64, PV_OFF["hy3"]:PV_OFF["hy3"] + 1], ["PV"]), fq, ALU.mult)
            k.tt(V(HYS.ap[0:64, 1:2], HYS.keys), V(PVt[0:64, PV_OFF["hy3"] + 2:PV_OFF["hy3"] + 3], ["PV"]), fq, ALU.mult)
            r64 = lambda v: V(v.ap[0:64, :], v.keys)
            p1 = psb(2)
            for th in range(2):
                k.mm(V(p1.ap[0:64, th * 512:(th + 1) * 512], p1.keys), V(W1.ap[0:33, :], W1.keys),
                     V(ZT.ap[0:33, th * 512:(th + 1) * 512], ZT.keys), True, True)
            k.ts(r64(A1), r64(p1), fq, ALU.mult, V(HYS.ap[0:64, 0:1], HYS.keys), ALU.add)
            range_reduce_sin(r64(H1), r64(A1), r64(TMP))
            p2 = psb(2)
            for th in range(2):
                k.mm(V(p2.ap[0:64, th * 512:(th + 1) * 512], p2.keys), V(W2.ap[0:64, :], W2.keys),
                     V(H1.ap[0:64, th * 512:(th + 1) * 512], H1.keys), True, True)
            k.ts(r64(A1), r64(p2), fq, ALU.mult, V(HYS.ap[0:64, 1:2], HYS.keys), ALU.add)
            H2 = SC.view(0, 1024)
            range_reduce_sin(r64(H2), r64(A1), r64(TMP))

            for g in range(4):
                W3 = SC.view(1024, 512); B3 = SC.view(1536, 512); DC = SC.view(2048, 512); DB = SC.view(2560, 512)
                WIN = SC.view(3072, 512); FT = SC.view(3584, 512)
                HS = SC.view(4608, 1024, BF16); HD = SC.view(5632, 1024, BF16)
                for hlf in range(2):
                    c0 = hlf * 1024 + g * 256
                    k.dma(V(W3.ap[0:64, hlf * 256:(hlf + 1) * 256], W3.keys), hy_w3[l, :, c0:c0 + 256], "sp", "hw3")
                    k.dma(V(B3.ap[0:1, hlf * 256:(hlf + 1) * 256], B3.keys), hy_b3[l:l + 1, c0:c0 + 256], "sp", "hb3")
                    k.dma(V(DC.ap[0:1, hlf * 256:(hlf + 1) * 256], DC.keys), hy_decay[l:l + 1, c0:c0 + 256], "sp", "hdc")
                k.act(V(DC.ap[0:1, :], DC.keys), V(DC.ap[0:1, :], DC.keys), AF.Abs)
                pd = psb()
                k.mm(pd, V(ONf.ap[0:1, :], ONf.keys), V(DC.ap[0:1, :], DC.keys), True, True)
                k.cp(DB, pd, eng="act")
                HS3 = HS.ap.rearrange("p (s c) -> p s c", s=8); HD3 = HD.ap.rearrange("p (s c) -> p s c", s=8)
                for tc in range(8):
                    pf = psb()
                    k.mm(pf, V(H2.ap[0:64, tc * 128:(tc + 1) * 128], H2.keys), V(W3.ap[0:64, :], W3.keys), True, False)
                    k.mm(pf, V(ONf.ap[0:1, :], ONf.keys), V(B3.ap[0:1, :], B3.keys), False, True)
                    k.act(WIN, DB, AF.Exp, scale=misc_c(MISC_NTN + tc))
                    k.tt(FT, pf, WIN, ALU.mult)
                    k.stt(HS.sub(HS3[:, tc, :]), FT.sub(FT.ap[:, 256:512]), misc_c(MISC_NS + tc),
                          FT.sub(FT.ap[:, 0:256]), ALU.mult, ALU.add)
                    k.stt(HD.sub(HD3[:, tc, :]), FT.sub(FT.ap[:, 256:512]), misc_c(MISC_NNS + tc),
                          FT.sub(FT.ap[:, 0:256]), ALU.mult, ALU.add)
                UT = SC.view(6656, 1024, BF16)
                UF = [SC.view(7680 + cc * 512, 512, BF16) for cc in range(2)]
                X0 = [SC.view(8704 + cc * 512, 512, BF16) for cc in range(2)]
                TA = SC.view(1024, 1024); TB = SC.view(2048, 1024)
                s, slot = next_slot()
                wv = load_cols(slot, w_in[l], [2048 + g * 256, 4096 + g * 256], 256)
                s2, slot2 = next_slot()
                wv2 = load_cols(slot2, w_in[l], [3072 + g * 256], 256)
                UT3 = UT.ap.rearrange("p (s c) -> p s c", s=8)
                for cc in range(2):
                    ch = g * 2 + cc
                    pv_ = proj_ps(wv, cc * 128)
                    conv_dw(TA, pv_, "conv_b_w", "conv_b_b", ch, 3, 1, WFB)
                    px1 = proj_ps(wv, 256 + cc * 128)
                    conv_dw(TB, px1, "conv_b_w", "conv_b_b", 16 + ch, 3, 1, WFB)
                    k.tt(UF[cc], TA, TB, ALU.mult)
                    px0 = proj_ps(wv2, cc * 128)
                    conv_dw(TA, px0, "conv_b_w", "conv_b_b", 8 + ch, 3, 1, WFB)
                    k.cp(X0[cc], TA, eng="act")
                    pt = psb()
                    ptb = V(pt.ap.bitcast(BF16), pt.keys)
                    for tc in range(8):
                        k.tr(ptb.sub(ptb.ap[:, tc * 128:(tc + 1) * 128]), UF[cc].sub(UF[cc].ap[:, tc * 128:(tc + 1) * 128]), IDb)
                    k.cp(UT.sub(UT3[:, :, cc * 128:(cc + 1) * 128]), ptb.sub(ptb.ap.rearrange("p (s c) -> p s c", s=8)),
                         eng="act")
                YR = SC.view(9728, 1024, BF16); YI = SC.view(10752, 1024, BF16)
                GC = SC.view(1024, 512); T1 = SC.view(1536, 256); T2 = SC.view(1792, 256)
                sC, slC = next_slot(); sS, slS = next_slot()
                FCv = V(slC.ap.rearrange("p (s n) -> p s n", s=8), slC.keys)
                FSv = V(slS.ap.rearrange("p (s n) -> p s n", s=8), slS.keys)
                k.dma(FCv, dft[0].rearrange("(s p) n -> p s n", p=128), "pool", "w" + slC.keys[0])
                k.dma(FSv, dft[1].rearrange("(s p) n -> p s n", p=128), "pool", "w" + slS.keys[0])
                YR3 = YR.ap.rearrange("p (s c) -> p s c", s=8); YI3 = YI.ap.rearrange("p (s c) -> p s c", s=8)
                for kc in range(8):
                    pg = psb(); pu = psb()
                    for (pp, o, M, src3, srcv) in ((pg, 0, FCv, HS3, HS), (pg, 256, FSv, HD3, HD),
                                                   (pu, 0, FCv, UT3, UT), (pu, 256, FSv, UT3, UT)):
                        for sc_ in range(8):
                            k.mm(pp.sub(pp.ap[:, o:o + 256]), M.sub(M.ap[:, sc_, kc * 128:(kc + 1) * 128]),
                                 srcv.sub(src3[:, sc_, :]), sc_ == 0, sc_ == 7)
                    k.cp(GC, pg, eng="act")
                    gr = GC.sub(GC.ap[:, 0:256]); gi_ = GC.sub(GC.ap[:, 256:512])
                    ur = pu.sub(pu.ap[:, 0:256]); ui = pu.sub(pu.ap[:, 256:512])
                    k.tt(T1, gr, ur, ALU.mult); k.tt(T2, gi_, ui, ALU.mult)
                    k.tt(YR.sub(YR3[:, kc, :]), T1, T2, ALU.subtract)
                    k.tt(T1, gr, ui, ALU.mult); k.tt(T2, gi_, ur, ALU.mult)
                    k.tt(YI.sub(YI3[:, kc, :]), T1, T2, ALU.add)
                sC, slC = next_slot(); sS, slS = next_slot()
                ICv = V(slC.ap.rearrange("p (s n) -> p s n", s=8), slC.keys)
                ISv = V(slS.ap.rearrange("p (s n) -> p s n", s=8), slS.keys)
                k.dma(ICv, dft[2].rearrange("(s p) n -> p s n", p=128), "pool", "w" + slC.keys[0])
                k.dma(ISv, dft[3].rearrange("(s p) n -> p s n", p=128), "pool", "w" + slS.keys[0])
                for cc in range(2):
                    ch = g * 2 + cc
                    for th in range(2):
                        py = psb()
                        for kc in range(8):
                            k.mm(py, YR.sub(YR3[:, kc, cc * 128:(cc + 1) * 128]), ICv.sub(ICv.ap[:, kc, th * 512:(th + 1) * 512]),
                                 kc == 0, False)
                        for kc in range(8):
                            k.mm(py, YI.sub(YI3[:, kc, cc * 128:(cc + 1) * 128]), ISv.sub(ISv.ap[:, kc, th * 512:(th + 1) * 512]),
                                 False, kc == 7)
                        t_ = SC.view(1536 + 0, 512)
                        k.stt(t_, UF[cc].sub(UF[cc].ap[:, th * 512:(th + 1) * 512]), pv("hy_bias", ch), py, ALU.mult, ALU.add)
                        k.tt(YB(ch, th), t_, X0[cc].sub(X0[cc].ap[:, th * 512:(th + 1) * 512]), ALU.mult)

            RG = SC.view(0, 2048, BF16)
            RG3 = V(RG.ap.rearrange("p (q n) -> p q n", q=32), RG.keys)
            k.dma(RG3, rgw[l, :, :].rearrange("p (q n) -> p q n", q=32), "pool", "rgw")
            U = SC.view(2048, 1024); UB = SC.view(3072, 512, BF16); GG = SC.view(3584, 1024); TG = SC.view(4608, 1024)
            Rr = SC.view(5632, 1024); Gi = SC.view(6656, 1024); Aa = SC.view(7680, 1024); Bb = SC.view(8704, 1024)
            Hd = [SC.view(9728, 1024), SC.view(10752, 1024)]
            for jp in range(4):
                s, slot = next_slot()
                wv = load_cols(slot, w_in[l], [jp * 256, 1024 + jp * 256], 256)
                for cc in range(2):
                    j = jp * 2 + cc
                    pxa = proj_ps(wv, cc * 128)
                    conv_dw(U, pxa, "conv_a_w", "conv_a_b", j, 4, 2, WFA)
                    k.cp(UB, U, eng="act")
                    pga = proj_ps(wv, 256 + cc * 128)
                    k.cp(GG, pga, eng="act")
                    k.tt(TG, GG, GG, ALU.mult)
                    k.ts(TG, TG, 0.044715, ALU.mult, 1.0, ALU.add)
                    k.tt(TG, TG, GG, ALU.mult)
                    k.act(TG, TG, AF.Sigmoid, scale=1.5957691216057308)
                    k.tt(GG, GG, TG, ALU.mult)
                    for d in range(2):
                        pr = psb(2); pi_ = psb(2)
                        for th in range(2):
                            k.mm(pr.sub(pr.ap[:, th * 512:(th + 1) * 512]), RG3.sub(RG3.ap[:, (d * 2) * 8 + j, :]),
                                 UB.sub(UB.ap[:, th * 512:(th + 1) * 512]), True, True)
                            k.mm(pi_.sub(pi_.ap[:, th * 512:(th + 1) * 512]), RG3.sub(RG3.ap[:, (d * 2 + 1) * 8 + j, :]),
                                 UB.sub(UB.ap[:, th * 512:(th + 1) * 512]), True, True)
                        k.act(Rr, pr, AF.Sigmoid, bias=pv("rg_br", d * 8 + j))
                        k.act(Gi, pi_, AF.Sigmoid, bias=pv("rg_bi", d * 8 + j))
                        k.act(Aa, Rr, AF.Exp, scale=V(C8.ap[:, d * 8 + j:d * 8 + j + 1], C8.keys))
                        k.tt(Bb, Aa, Aa, ALU.mult)
                        k.ts(Bb, Bb, -1.0, ALU.mult, 1.0, ALU.add)
                        k.ts(Bb, Bb, 0.0, ALU.max)
                        k.act(Bb, Bb, AF.Sqrt)
                        k.tt(Gi, Gi, U, ALU.mult)
                        k.tt(Bb, Bb, Gi, ALU.mult)
                        edge = 0 if d == 0 else 1023
                        h0c = misc_c(MISC_H0 + l * 16 + d * 8 + j)
                        k.stt(Bb.sub(Bb.ap[:, edge:edge + 1]), Aa.sub(Aa.ap[:, edge:edge + 1]), h0c,
                              Bb.sub(Bb.ap[:, edge:edge + 1]), ALU.mult, ALU.add)
                        a4 = Aa.ap.rearrange("p (s t) -> p s t", s=4)
                        ecol = a4[:, :, 0:1] if d == 0 else a4[:, :, 255:256]
                        k.ts(Aa.sub(ecol), Aa.sub(ecol), flag, ALU.mult)
                        if d == 0:
                            k.scan(Hd[0], Aa, Bb)
                        else:
                            k.scan(Hd[1].sub(Hd[1].ap[:, ::-1]), Aa.sub(Aa.ap[:, ::-1]), Bb.sub(Bb.ap[:, ::-1]))
                        h4 = Hd[d].ap.rearrange("p (s t) -> p s t", s=4)
                        fin = h4[:, :, 255:256] if d == 0 else h4[:, :, 0:1]
                        so = ((l * 2 + d) * 8 + j) * 4
                        k.cp(V(SOUTv.ap[:, so:so + 4].rearrange("p (s o) -> p s o", o=1), SOUTv.keys), Hd[d].sub(fin))
                    k.tt(Hd[0], Hd[0], Hd[1], ALU.add)
                    k.tt(YA(j), Hd[0], GG, ALU.mult)

            SA = SC.view(0, 512); SB_ = SC.view(512, 512); M1 = SC.view(1024, 512); M2 = SC.view(1536, 512)
            for half in range(2):
                sa, slA = next_slot(); sb_, slB = next_slot()
                PA = V(slA.ap.rearrange("p (j n) -> p j n", j=8), slA.keys)
                PB = V(slB.ap.rearrange("p (j n) -> p j n", j=8), slB.keys)
                k.dma(PA, w_proj_a[l].rearrange("(j p) n -> p j n", p=128)[:, :, half * 1024:(half + 1) * 1024], "pool", "w" + slA.keys[0])
                k.dma(PB, w_proj_b[l].rearrange("(j p) n -> p j n", p=128)[:, :, half * 1024:(half + 1) * 1024], "pool", "w" + slB.keys[0])
                for ip in range(4):
                    gs = 3 - sa - sb_
                    slG = wslot(gs)
                    i0 = half * 8 + ip * 2
                    wg = load_cols(slG, w_gate[l], [i0 * 128, 2048 + i0 * 128], 256)
                    for cc in range(2):
                        i = i0 + cc
                        for th in range(2):
                            pga = psb(); pgb = psb(); ppa = psb(); ppb = psb()
                            for kk in range(16):
                                k.mm(pga, wg.sub(wg.ap[:, kk, cc * 128:(cc + 1) * 128]), htv(kk, th), kk == 0, kk == 15)
                            for kk in range(16):
                                k.mm(pgb, wg.sub(wg.ap[:, kk, 256 + cc * 128:256 + (cc + 1) * 128]), htv(kk, th), kk == 0, kk == 15)
                            il = (i - half * 8) * 128
                            for jj in range(8):
                                k.mm(ppa, PA.sub(PA.ap[:, jj, il:il + 128]), YA(jj, th), jj == 0, jj == 7)
                            for jj in range(8):
                                k.mm(ppb, PB.sub(PB.ap[:, jj, il:il + 128]), YB(jj, th), jj == 0, jj == 7)
                            k.act(SA, pga, AF.Sigmoid, bias=pv("b_gate", i))
                            k.act(SB_, pgb, AF.Sigmoid, bias=pv("b_gate", 16 + i))
                            k.tt(M1, SA, ppa, ALU.mult)
                            k.tt(M2, SB_, ppb, ALU.mult)
                            k.tt(MG(i, th), M1, M2, ALU.add)
                wb_ctr[0] = 0
            for c in range(16):
                for th in range(2):
                    k.cp(htv(c, th), MG(c, th), eng="act")
            k.dma(V(Xt[:], [f"X.{u}" for u in range(32)]), xsp[:, :], "sp", "spill", rk=["xsp"])
            for cb in range(4):
                s, slot = next_slot()
                wv = load_cols(slot, w_out[l], [cb * 512], 512)
                for ci in range(4):
                    c = cb * 4 + ci
                    for th in range(2):
                        po = psb()
                        for kk in range(16):
                            k.mm(po, wv.sub(wv.ap[:, kk, ci * 128:(ci + 1) * 128]), htv(kk, th), kk == 0, kk == 15)
                        x = xv(c, th)
                        k.ts(x, x, ALPHA, ALU.mult, V(G1B.ap[:, c:c + 1], G1B.keys), ALU.add)
                        k.stt(x, po, V(g1.ap[:, c:c + 1], g1.keys), x, ALU.mult, ALU.add)
            layer_norm("ln1_g", "ln1_b")

        def moe(l):
            sh2 = V(MOD.ap[:, 48:64], MOD.keys)
            g2 = V(MOD.ap[:, 80:96], MOD.keys)
            modulate(SC2P, sh2)
            k.dma(RW, router_w[l].rearrange("(k p) n -> p k n", p=128), "pool", "rw")
            BD = SC.view(0, 2048)
            GT = SC.view(2048, 1024)
            k.dma(V(BD.ap[0:32, :], BD.keys), b_down[l, :, :], "sp", "bd")
            LG = V(SMALL.ap[:, 0:32], SMALL.keys); M8 = V(SMALL.ap[:, 32:40], SMALL.keys)
            NM = V(SMALL.ap[:, 40:41], SMALL.keys); SM = V(SMALL.ap[:, 41:42], SMALL.keys)
            EX = SC.view(3072, 32); MK = SC.view(3328, 32)
            G3 = GTOK.ap.rearrange("p (t e) -> p t e", t=8)
            for tt in range(8):
                pl = psb()
                for kk in range(16):
                    k.mm(pl.sub(pl.ap[:, 0:32]), htv(kk).sub(htv(kk).ap[:, tt * 128:(tt + 1) * 128]), RW.sub(RW.ap[:, kk, :]),
                         kk == 0, kk == 15)
                k.tt(LG, pl.sub(pl.ap[:, 0:32]), pv("router_b", 0, 32), ALU.add)
                k.max8(M8, LG)
                k.ts(MK, LG, V(M8.ap[:, 3:4], M8.keys), ALU.is_ge)
                k.ts(NM, V(M8.ap[:, 0:1], M8.keys), -1.0, ALU.mult)
                k.act(EX, LG, AF.Exp, bias=NM)
                k.tt(EX, EX, MK, ALU.mult)
                k.rsum(SM, EX)
                k.recip(SM, SM)
                k.ts(GTOK.sub(G3[:, tt, :]), EX, SM, ALU.mult)
                pT = psb()
                k.tr(V(pT.ap[0:32, 0:128], pT.keys), GTOK.sub(G3[:, tt, :]), IDf)
                k.cp(V(GT.ap[0:32, tt * 128:(tt + 1) * 128], GT.keys), V(pT.ap[0:32, 0:128], pT.keys), eng="act")
            for c in range(16):
                for th in range(2):
                    pb_ = psb()
                    k.mm(pb_, V(BD.ap[0:32, c * 128:(c + 1) * 128], BD.keys), V(GT.ap[0:32, th * 512:(th + 1) * 512], GT.keys),
                         True, True)
                    x = xv(c, th)
                    k.ts(x, x, ALPHA, ALU.mult)
                    k.stt(x, pb_, V(g2.ap[:, c:c + 1], g2.keys), x, ALU.mult, ALU.add)
            ACTT = lambda m, th: SC.view(m * 512 + th * 256, 256, BF16)
            GB = SC.view(4096, 1024)
            DG = [SC.view(5120, 128), SC.view(5248, 128)]
            TMPS = [[SC.view(5632 + s_ * 1536 + i * 512, 512) for i in range(3)] for s_ in range(2)]
            BU1 = SC.view(8704, 1024)
            k.ts(BU1, pv("b_gu", 0, 1024), 1.0, ALU.add)
            blk = 0
            for e in range(n_exp):
                pgb_ = psb(2)
                for tt in range(8):
                    dg = DG[tt % 2]
                    k.ts(dg, IDf, V(G3[:, tt, e:e + 1], GTOK.keys), ALU.mult)
                    k.mm(pgb_.sub(pgb_.ap[:, tt * 128:(tt + 1) * 128]), ONf, dg, True, True)
                k.cp(GB, pgb_, eng="act")
                for hf in range(2):
                    if l + 1 < n_layers and e >= 1:
                        mod_step(l + 1)
                    for sp_ in range(4):
                        s, slot = next_slot()
                        m0 = hf * 8 + sp_ * 2
                        wv = load_cols(slot, w_gu[l, e], [m0 * 128, 2048 + m0 * 128], 256)
                        for cc in range(2):
                            m = m0 + cc
                            ml = sp_ * 2 + cc
                            for th in range(2):
                                pg = psb(); pu = psb()
                                for kk in range(16):
                                    k.mm(pg, wv.sub(wv.ap[:, kk, cc * 128:(cc + 1) * 128]), htv(kk, th), kk == 0, kk == 15)
                                for kk in range(16):
                                    k.mm(pu, wv.sub(wv.ap[:, kk, 256 + cc * 128:256 + (cc + 1) * 128]), htv(kk, th), kk == 0, kk == 15)
                                tg, tsg, tl = TMPS[blk % 2]
                                blk += 1
                                k.ts(tg, pg, pv("b_gu", e * 32 + m), ALU.add, 7.0, ALU.min)
                                k.act(tsg, tg, AF.Sigmoid, scale=1.702)
                                k.act(tl, pu, AF.Identity, bias=V(BU1.ap[:, e * 32 + 16 + m:e * 32 + 16 + m + 1], BU1.keys))
                                k.ts(tl, tl, -6.0, ALU.max, 8.0, ALU.min)
                                k.tt(tg, tg, tsg, ALU.mult)
                                k.tt(tl, tl, GB.sub(GB.ap[:, th * 512:(th + 1) * 512]), ALU.mult)
                                k.tt(ACTT(ml, th), tg, tl, ALU.mult)
                    dsl = []
                    for sd in range(2):
                        s, slot = next_slot()
                        r0 = (hf * 8 + sd * 4) * 128
                        dv = V(slot.ap.rearrange("p (j a b) -> p j a b", j=4, a=4), slot.keys)
                        k.dma(dv, w_down[l, e, r0:r0 + 512, :].rearrange("(j p) (a b) -> p j a b", p=128, b=512),
                              "pool", "w" + slot.keys[0])
                        dsl.append(V(slot.ap.rearrange("p (j n) -> p j n", j=4), slot.keys))
                    for c in range(16):
                        for th in range(2):
                            po = psb()
                            for ml in range(8):
                                dv = dsl[ml // 4]
                                k.mm(po, dv.sub(dv.ap[:, ml % 4, c * 128:(c + 1) * 128]), ACTT(ml, th), ml == 0, ml == 7)
                            x = xv(c, th)
                            k.stt(x, po, V(g2.ap[:, c:c + 1], g2.keys), x, ALU.mult, ALU.add)
            layer_norm("ln2_g", "ln2_b")

        mod_state = {}

        def mod_begin(l):
            MODx, SC1Px, SC2Px, BMx = MODS[l % 2]
            k.dma(BMx, pvec[l, :, PV_OFF["b_mod"]:PV_OFF["b_mod"] + 96], "sp", f"bm{l % 2}")
            mod_state[l] = 0

        def mod_step(l):
            j = mod_state.get(l)
            if j is None or j >= 24:
                return
            MODx, SC1Px, SC2Px, BMx = MODS[l % 2]
            mod_ps = PS.view(7 * 512, 512)
            s_, slot = next_slot()
            wv = load_cols(slot, w_mod[l], [j * 512], 512)
            for cc in range(4):
                col = j * 4 + cc
                for kk in range(16):
                    k.mm(mod_ps.sub(mod_ps.ap[:, col:col + 1]), wv.sub(wv.ap[:, kk, cc * 128:(cc + 1) * 128]),
                         V(SCONDb.ap[:, kk:kk + 1], SCONDb.keys), kk == 0, kk == 15)
            mod_state[l] = j + 1
            if j == 23:
                k.tt(MODx, mod_ps.sub(mod_ps.ap[:, 0:96]), BMx, ALU.add)
                k.ts(SC1Px, V(MODx.ap[:, 16:32], MODx.keys), 1.0, ALU.add)
                k.ts(SC2Px, V(MODx.ap[:, 64:80], MODx.keys), 1.0, ALU.add)

        mod_begin(0)
        for _ in range(24):
            mod_step(0)
        for l in range(n_layers):
            MOD, SC1P, SC2P = MODS[l % 2][:3]
            k.dma(PV, pvec[l, :, :], "sp", "pv")
            if l + 1 < n_layers:
                mod_begin(l + 1)
            k.tt(G1B, V(MOD.ap[:, 32:48], MOD.keys), pv("b_out", 0, 16), ALU.mult)
            if do_mixer:
                mixer(l)
            if do_moe:
                moe(l)
            if l + 1 < n_layers:
                while mod_state[l + 1] < 24:
                    mod_step(l + 1)

        for tt in range(8):
            ot = SC.view((tt % 2) * 2048, 2048)
            for cb in range(4):
                pb = psb()
                for ci in range(4):
                    c = cb * 4 + ci
                    xc = xv(c, tt // 4)
                    k.tr(pb.sub(pb.ap[:, ci * 128:(ci + 1) * 128]), xc.sub(xc.ap[:, (tt % 4) * 128:(tt % 4 + 1) * 128]), IDf)
                k.cp(ot.sub(ot.ap[:, cb * 512:(cb + 1) * 512]), pb, eng="act")
            k.dma(yout[tt * 128:(tt + 1) * 128, :], ot, "sp", f"st{tt % 2}", is_out=True)
        k.dma(sout[:, :], SOUTv, "sp", "sts", is_out=True)
        P.emit()
    return nc


_CACHE = {}


def kernel(**inp):
    inp = {k_: np.asarray(v) for k_, v in inp.items()}
    n_layers = _CACHE.get("n_layers", NL)
    key = ("nc", n_layers, _CACHE.get("do_mixer", True), _CACHE.get("do_moe", True), _CACHE.get("n_exp", NE))
    nc = build(n_layers, key[2], key[3], key[4])
    pvec = np.stack([pack_pvec(inp, l) for l in range(NL)])
    rgw = np.stack([pack_rgw(inp, l).reshape(128, 32 * 128) for l in range(NL)])
    ident = np.eye(128, dtype=np.float32)
    gp = grid_pos()
    zeros_pos = np.zeros((T, D), np.float32)
    tabs = {Ls: seg_tables(Ls) for Ls in (1024, 256)}
    shared = dict(ident=ident, pvec=pvec, rgw=rgw)
    for n in ("w_mod", "w_in", "hy_w1", "hy_w2", "hy_w3", "hy_b3", "hy_decay", "w_proj_a", "w_proj_b", "w_gate",
              "w_out", "router_w", "w_gu", "w_down", "b_down"):
        shared[n] = np.ascontiguousarray(inp[n], dtype=np.float32)
    in_maps = []
    for core in range(8):
        sample = core < 4
        Ls = 1024 if sample else 256
        zT, tn, ns, dft = tabs[Ls]
        m = dict(shared)
        misc = np.zeros((128, NMISC), np.float32)
        if sample:
            b = core
            m["xin"] = np.ascontiguousarray(inp["x_sample"][b])
            m["pos"] = gp
            m["cond"] = _fm(inp["c"][b])
            misc[:, MISC_FLAG] = 1.0
            st = inp["state_rglru"][b]
            for l in range(NL):
                for d in range(2):
                    misc[:, MISC_H0 + l * 16 + d * 8:MISC_H0 + l * 16 + d * 8 + 8] = _fm(st[l, d])
        else:
            p0 = (core - 4) * 4
            m["xin"] = np.ascontiguousarray(inp["x_prompt"][p0:p0 + 4].reshape(T, D))
            m["pos"] = zeros_pos
            m["cond"] = _fm(inp["c_ctx"])
        misc[:, MISC_NTN:MISC_NTN + 8] = -_fm(tn)
        misc[:, MISC_NS:MISC_NS + 8] = _fm(ns)
        misc[:, MISC_NNS:MISC_NNS + 8] = -_fm(ns)
        m["misc"] = misc
        m["zT"] = zT
        m["dft"] = dft
        in_maps.append(m)
    res = run_bass_kernel_spmd(nc, in_maps, core_ids=list(range(8)))
    outs = res.results
    y_sample = np.stack([outs[c]["yout"] for c in range(4)]).astype(np.float32)
    y_prompt = np.concatenate([outs[c]["yout"].reshape(4, 256, D) for c in range(4, 8)]).astype(np.float32)
    ns_ = np.zeros((16, NL, 2, 1024), np.float32)
    for c in range(4, 8):
        so = outs[c]["sout"].reshape(128, NL, 2, 8, 4)
        ns_[(c - 4) * 4:(c - 4) * 4 + 4] = so.transpose(4, 1, 2, 3, 0).reshape(4, NL, 2, 1024)
    _CACHE["last"] = outs
    return (y_prompt, y_sample, ns_)
```
